# Optimizing a Trainium2 kernel written in Bass

```python
import math
import jax
import jax.numpy as jnp
from jax import lax
import numpy as np

D_MODEL = 1024
BATCH = 8
SEQ = 2048
DEPTH = 4

GRID_W = 64
CTX_LEN = 256
N_MIXERS = 4
GROUP_W = D_MODEL // N_MIXERS
A_HEADS = 4
A_VDIM = GROUP_W // A_HEADS
A_QKDIM = A_VDIM // 2
B_HEADS = 4
B_DIM = GROUP_W // B_HEADS
C_KSIZE = 31
POOL_WINDOWS = (2, 4, 8, 16)
POOL_GROUP = GROUP_W // len(POOL_WINDOWS)
D_FF = 2816
FFN_KSIZE = 3
RET_CHUNK = 128
Q_BLOCK = 128
ROPE_BASE = 10000.0
EPS = 1e-6
N_MOD = 6
OFF_AQ = 0
OFF_BQ = OFF_AQ + GROUP_W
OFF_BG = OFF_BQ + GROUP_W
OFF_C = OFF_BG + GROUP_W
OFF_D = OFF_C + 2 * GROUP_W
OFF_KV = OFF_D + GROUP_W
KV_AK = 0
KV_AV = GROUP_W
KV_BK = 2 * GROUP_W
KV_BV = 3 * GROUP_W
KV_W = 4 * GROUP_W
D_IN = OFF_KV + KV_W

kernel_name = 'hybrid_diffusion_parallel_mixer_block'


def rms_norm(x, g):
    xf = x.astype(jnp.float32)
    y = xf * lax.rsqrt(jnp.mean(xf * xf, axis=-1, keepdims=True) + EPS)
    return (y * g.astype(jnp.float32)).astype(x.dtype)


def head_rms(x, g=None):
    xf = x.astype(jnp.float32)
    y = xf * lax.rsqrt(jnp.mean(xf * xf, axis=-1, keepdims=True) + EPS)
    return y if g is None else y * g.astype(jnp.float32)


def layer_norm(x, g, b):
    xf = x.astype(jnp.float32)
    mu = jnp.mean(xf, axis=-1, keepdims=True)
    var = jnp.mean(jnp.square(xf - mu), axis=-1, keepdims=True)
    y = (xf - mu) * lax.rsqrt(var + EPS)
    return (y * g.astype(jnp.float32) + b.astype(jnp.float32)).astype(x.dtype)


def modulate(h, shift, scale):
    return h * (1 + scale) + shift


def rope_2d(x, rows, cols):
    n, d = x.shape[1], x.shape[-1]
    q = d // 4
    inv = ROPE_BASE ** (-jnp.arange(q, dtype=jnp.float32) / q)
    ang = jnp.stack([rows[:, None] * inv, cols[:, None] * inv], axis=1)
    bshape = (n,) + (1,) * (x.ndim - 3) + (2, q)
    cos = jnp.cos(ang).reshape(bshape).astype(x.dtype)
    sin = jnp.sin(ang).reshape(bshape).astype(x.dtype)
    xr = x.reshape(x.shape[:-1] + (2, 2, q))
    a, b = xr[..., 0, :], xr[..., 1, :]
    return jnp.stack([a * cos - b * sin, a * sin + b * cos], axis=-2).reshape(x.shape)


def diff_attention(q, k, v, lam):
    bsz, nq = q.shape[0], q.shape[1]
    scale = A_QKDIM ** -0.5

    def attend(qb):
        s = jnp.einsum('bqhmd,bkhmd->bhmqk', qb, k).astype(jnp.float32) * scale
        p = jax.nn.softmax(s, axis=-1)
        w = p[:, :, 0] - lam * p[:, :, 1]
        return jnp.einsum('bhqk,bkhd->bqhd', w.astype(v.dtype), v)

    nb = nq // Q_BLOCK
    qb = q.reshape((bsz, nb, Q_BLOCK) + q.shape[2:]).swapaxes(0, 1)
    o = lax.map(attend, qb)
    return o.swapaxes(0, 1).reshape(bsz, nq, o.shape[-2], o.shape[-1])


def retention_scan(q, k, v, log_gamma, s0):
    bsz, n, h, dk = q.shape
    dv = v.shape[-1]
    nc = n // RET_CHUNK
    pos = jnp.arange(RET_CHUNK, dtype=jnp.float32)
    lg = log_gamma[:, None]
    rel = pos[:, None] - pos[None, :]
    intra = jnp.where(rel >= 0, jnp.exp(jnp.maximum(rel, 0.0)[None] * lg[:, :, None]), 0.0)
    q_dec = jnp.exp((pos + 1.0)[None] * lg).T[:, :, None]
    k_dec = jnp.exp((RET_CHUNK - 1.0 - pos)[None] * lg).T[:, :, None]
    c_dec = jnp.exp(RET_CHUNK * log_gamma)[:, None, None]
    to_chunks = lambda t: t.reshape(bsz, nc, RET_CHUNK, h, t.shape[-1]).swapaxes(0, 1)

    def step(state, inp):
        qc, kc, vc = inp
        scores = jnp.einsum('bihd,bjhd->bhij', qc, kc) * intra
        out = (jnp.einsum('bhij,bjhe->bihe', scores, vc)
               + jnp.einsum('bihd,bhde->bihe', qc * q_dec, state))
        state = c_dec * state + jnp.einsum('bjhd,bjhe->bhde', kc * k_dec, vc)
        return state, out

    state, out = lax.scan(step, s0, (to_chunks(q), to_chunks(k), to_chunks(v)))
    return out.swapaxes(0, 1).reshape(bsz, n, h, dv), state


def retention_state(k, v, log_gamma):
    n = k.shape[1]
    w = jnp.exp((n - 1.0 - jnp.arange(n, dtype=jnp.float32))[:, None] * log_gamma[None, :])
    return jnp.einsum('bnhd,bnhe->bhde', k * w[None, :, :, None], v)


def dwconv(x, w, b):
    ksz = w.shape[0]
    y = lax.conv_general_dilated(x, w[:, None, :], window_strides=(1,),
                                 padding=[(ksz // 2, ksz // 2)],
                                 dimension_numbers=('NWC', 'WIO', 'NWC'),
                                 feature_group_count=x.shape[-1])
    return y + b


def conformer_conv(u2, w_dw, b_dw, ln_g, ln_b):
    a, g = jnp.split(u2, 2, axis=-1)
    u = a * jax.nn.sigmoid(g)
    y = layer_norm(dwconv(u, w_dw, b_dw), ln_g, ln_b)
    return jax.nn.silu(y)


def multiscale_pool(u, w_lin, scale):
    bsz, n, _ = u.shape
    t = jnp.arange(n)
    ug = u.reshape(bsz, n, len(POOL_WINDOWS), POOL_GROUP)
    cs = jnp.concatenate([jnp.zeros((bsz, 1) + ug.shape[2:], jnp.float32),
                          jnp.cumsum(ug.astype(jnp.float32), axis=1)], axis=1)
    outs = []
    for gi, win in enumerate(POOL_WINDOWS):
        lo = jnp.clip(t - win // 2, 0, n)
        hi = jnp.clip(t + win // 2, 0, n)
        csg = cs[:, :, gi]
        mean = (csg[:, hi] - csg[:, lo]) / (hi - lo).astype(jnp.float32)[None, :, None]
        outs.append(mean - ug[:, :, gi].astype(jnp.float32))
    pooled = jnp.stack(outs, axis=2).astype(u.dtype)
    y = jnp.einsum('bngc,gcd->bngd', pooled, w_lin).reshape(bsz, n, GROUP_W)
    return y * scale


def conv_ffn(h, w_up, w_dw, b_dw, w_down):
    u = dwconv(h @ w_up, w_dw, b_dw)
    a, g = jnp.split(u, 2, axis=-1)
    return (a * jax.nn.silu(g)) @ w_down


def token_mixers(hx, hc, w_in, w_out, lam_p, subln_g, ret_decay, cdw_w, cdw_b, cln_g, cln_b,
                 pool_w, pool_scale, rows, cols, lambda_init, need_ctx):
    bsz, n, _ = hx.shape
    px = hx @ w_in
    pc = hc @ (w_in if need_ctx else w_in[:, OFF_KV:])
    kvx = px[..., OFF_KV:]
    kvc = pc[..., pc.shape[-1] - KV_W:]
    sl = lambda t, off, w: t[..., off:off + w]
    qk_a = lambda t: t.reshape(t.shape[0], t.shape[1], A_HEADS, 2, A_QKDIM)
    v_a = lambda t: t.reshape(t.shape[0], t.shape[1], A_HEADS, A_VDIM)
    h_b = lambda t: t.reshape(t.shape[0], t.shape[1], B_HEADS, B_DIM)
    flip = lambda t: t[:, ::-1]

    lam = (jnp.exp(jnp.sum(lam_p[0] * lam_p[1]).astype(jnp.float32))
           - jnp.exp(jnp.sum(lam_p[2] * lam_p[3]).astype(jnp.float32)) + lambda_init)
    ak_c = qk_a(sl(kvc, KV_AK, GROUP_W))
    av_c = v_a(sl(kvc, KV_AV, GROUP_W))
    aq_x = rope_2d(qk_a(sl(px, OFF_AQ, GROUP_W)), rows, cols)
    ak_x = rope_2d(qk_a(sl(kvx, KV_AK, GROUP_W)), rows, cols)
    av_x = v_a(sl(kvx, KV_AV, GROUP_W))

    def diff_out(o):
        y = head_rms(o, subln_g) * (1.0 - lambda_init)
        return y.reshape(o.shape[0], o.shape[1], GROUP_W).astype(hx.dtype)

    ya_x = diff_out(diff_attention(aq_x, jnp.concatenate([ak_c, ak_x], axis=1),
                                   jnp.concatenate([av_c, av_x], axis=1), lam))

    lg_f = -jnp.exp(ret_decay[0].astype(jnp.float32))
    lg_b = -jnp.exp(ret_decay[1].astype(jnp.float32))
    bq_x = rope_2d(h_b(sl(px, OFF_BQ, GROUP_W)), rows, cols)
    bk_x = rope_2d(h_b(sl(kvx, KV_BK, GROUP_W)), rows, cols) * (B_DIM ** -0.5)
    bv_x = h_b(sl(kvx, KV_BV, GROUP_W))
    bk_c = h_b(sl(kvc, KV_BK, GROUP_W)) * (B_DIM ** -0.5)
    bv_c = h_b(sl(kvc, KV_BV, GROUP_W))
    if need_ctx:
        bq_c = h_b(sl(pc, OFF_BQ, GROUP_W))
        zeros = jnp.zeros((bsz, B_HEADS, B_DIM, B_DIM), jnp.float32)
        oc_f, s_f = retention_scan(bq_c, bk_c, bv_c, lg_f, zeros)
        oc_b, s_b = retention_scan(flip(bq_c), flip(bk_c), flip(bv_c), lg_b, zeros)
        ob_c = oc_f + flip(oc_b)
    else:
        s_f = retention_state(bk_c, bv_c, lg_f)
        s_b = retention_state(flip(bk_c), flip(bv_c), lg_b)
    ox_f, _ = retention_scan(bq_x, bk_x, bv_x, lg_f, s_f)
    ox_b, _ = retention_scan(flip(bq_x), flip(bk_x), flip(bv_x), lg_b, s_b)

    def ret_out(o, gate):
        y = jax.nn.silu(gate.astype(jnp.float32)) * head_rms(o).reshape(gate.shape)
        return y.astype(hx.dtype)

    yb_x = ret_out(ox_f + flip(ox_b), sl(px, OFF_BG, GROUP_W))

    yc_x = conformer_conv(sl(px, OFF_C, 2 * GROUP_W), cdw_w, cdw_b, cln_g, cln_b).astype(hx.dtype)
    yd_x = multiscale_pool(sl(px, OFF_D, GROUP_W), pool_w, pool_scale).astype(hx.dtype)
    yx = (jnp.concatenate([ya_x, yb_x, yc_x, yd_x], axis=-1) @ w_out).astype(hx.dtype)
    if not need_ctx:
        return yx, None

    ya_c = diff_out(diff_attention(qk_a(sl(pc, OFF_AQ, GROUP_W)), ak_c, av_c, lam))
    yb_c = ret_out(ob_c, sl(pc, OFF_BG, GROUP_W))
    yc_c = conformer_conv(sl(pc, OFF_C, 2 * GROUP_W), cdw_w, cdw_b, cln_g, cln_b).astype(hc.dtype)
    yd_c = multiscale_pool(sl(pc, OFF_D, GROUP_W), pool_w, pool_scale).astype(hc.dtype)
    yc = (jnp.concatenate([ya_c, yb_c, yc_c, yd_c], axis=-1) @ w_out).astype(hc.dtype)
    return yx, yc


def setup_inputs(seed: int = 0) -> dict:
    key = jax.random.key(seed)
    ks = jax.random.split(key, 24)
    nrm = lambda k, shape, s: jax.random.normal(k, shape, jnp.float32) * s
    ret_base = jnp.log(-jnp.log1p(-(2.0 ** (-5.0 - jnp.arange(B_HEADS, dtype=jnp.float32)))))
    return {
        'x': nrm(ks[0], (BATCH, SEQ, D_MODEL), 1.0),
        'c': nrm(ks[1], (BATCH, D_MODEL), 1.0),
        'ctx': nrm(ks[2], (BATCH, CTX_LEN, D_MODEL), 1.0),
        'c_ctx': nrm(ks[3], (D_MODEL,), 1.0),
        'w_mod': nrm(ks[4], (DEPTH, D_MODEL, N_MOD * D_MODEL), 0.5 * D_MODEL ** -0.5),
        'b_mod': nrm(ks[5], (DEPTH, N_MOD * D_MODEL), 0.02),
        'norm1_g': 1.0 + nrm(ks[6], (DEPTH, D_MODEL), 0.02),
        'norm2_g': 1.0 + nrm(ks[7], (DEPTH, D_MODEL), 0.02),
        'w_in': nrm(ks[8], (DEPTH, D_MODEL, D_IN), D_MODEL ** -0.5),
        'w_out': nrm(ks[9], (DEPTH, N_MIXERS * GROUP_W, D_MODEL), (N_MIXERS * GROUP_W) ** -0.5),
        'diff_lambda': nrm(ks[10], (DEPTH, 4, A_QKDIM), 0.1),
        'diff_subln_g': 1.0 + nrm(ks[11], (DEPTH, A_VDIM), 0.02),
        'ret_decay': ret_base[None, None, :] + nrm(ks[12], (DEPTH, 2, B_HEADS), 0.05),
        'conv_dw_w': nrm(ks[13], (DEPTH, C_KSIZE, GROUP_W), C_KSIZE ** -0.5),
        'conv_dw_b': nrm(ks[14], (DEPTH, GROUP_W), 0.02),
        'conv_ln_g': 1.0 + nrm(ks[15], (DEPTH, GROUP_W), 0.02),
        'conv_ln_b': nrm(ks[16], (DEPTH, GROUP_W), 0.02),
        'pool_w': nrm(ks[17], (DEPTH, len(POOL_WINDOWS), POOL_GROUP, POOL_GROUP), POOL_GROUP ** -0.5),
        'pool_scale': 1.0 + nrm(ks[18], (DEPTH, GROUP_W), 0.02),
        'ffn_w_up': nrm(ks[19], (DEPTH, D_MODEL, 2 * D_FF), D_MODEL ** -0.5),
        'ffn_dw_w': nrm(ks[20], (DEPTH, FFN_KSIZE, 2 * D_FF), FFN_KSIZE ** -0.5),
        'ffn_dw_b': nrm(ks[21], (DEPTH, 2 * D_FF), 0.02),
        'ffn_w_down': nrm(ks[22], (DEPTH, D_FF, D_MODEL), D_FF ** -0.5),
        'final_g': 1.0 + nrm(ks[23], (D_MODEL,), 0.02),
    }


def reference(x, c, ctx, c_ctx, w_mod, b_mod, norm1_g, norm2_g, w_in, w_out, diff_lambda,
              diff_subln_g, ret_decay, conv_dw_w, conv_dw_b, conv_ln_g, conv_ln_b, pool_w,
              pool_scale, ffn_w_up, ffn_dw_w, ffn_dw_b, ffn_w_down, final_g):
    n = x.shape[1]
    n_rows = n // GRID_W
    rows = jnp.repeat(jnp.arange(n_rows), GRID_W).astype(jnp.float32)
    cols = jnp.tile(jnp.arange(GRID_W), n_rows).astype(jnp.float32)
    sc = jax.nn.silu(c)
    scc = jax.nn.silu(c_ctx)
    for l in range(DEPTH):
        need_ctx = l < DEPTH - 1
        lambda_init = 0.8 - 0.6 * math.exp(-0.3 * l)
        mx = jnp.split(sc @ w_mod[l] + b_mod[l], N_MOD, axis=-1)
        mx = [m[:, None, :] for m in mx]
        mc = jnp.split(scc @ w_mod[l] + b_mod[l], N_MOD, axis=-1)
        hx = modulate(rms_norm(x, norm1_g[l]), mx[0], mx[1])
        hc = modulate(rms_norm(ctx, norm1_g[l]), mc[0], mc[1])
        yx, yc = token_mixers(hx, hc, w_in[l], w_out[l], diff_lambda[l], diff_subln_g[l],
                              ret_decay[l], conv_dw_w[l], conv_dw_b[l], conv_ln_g[l], conv_ln_b[l],
                              pool_w[l], pool_scale[l], rows, cols, lambda_init, need_ctx)
        x = x + mx[2] * yx
        hx = modulate(rms_norm(x, norm2_g[l]), mx[3], mx[4])
        x = x + mx[5] * conv_ffn(hx, ffn_w_up[l], ffn_dw_w[l], ffn_dw_b[l], ffn_w_down[l])
        if need_ctx:
            ctx = ctx + mc[2] * yc
            hc = modulate(rms_norm(ctx, norm2_g[l]), mc[3], mc[4])
            ctx = ctx + mc[5] * conv_ffn(hc, ffn_w_up[l], ffn_dw_w[l], ffn_dw_b[l], ffn_w_down[l])
    return rms_norm(x, final_g)
```

```python
import math
import numpy as np
from contextlib import ExitStack
import concourse.bass as bass
import concourse.mybir as mybir
from concourse.bass_utils import run_bass_kernel_spmd

F32 = mybir.dt.float32
BF16 = mybir.dt.bfloat16
ALU = mybir.AluOpType
AF = mybir.ActivationFunctionType
AX = mybir.AxisListType

D = 1024
T = 2048
TC = 256
NT = T + TC
NTILE = NT // 128
DIN = 2560
DFF = 2816
DEPTH = 4
DGE_SCRATCH = 4096
EPS = 1e-6
NPF = 262
NPB = 200
FFN_GROUPS = [3, 3, 3, 3, 3, 3, 2, 2]

SAME_ENGINE_RAW = True


class Instr:
    __slots__ = ("eng", "fn", "deps", "idx", "is_dma", "sig", "count", "lane", "phase")

    def __init__(self, eng, fn, is_dma):
        self.eng = eng
        self.fn = fn
        self.deps = []
        self.is_dma = is_dma
        self.sig = False
        self.count = 0
        self.lane = None


class Prog:
    ENGS = ("pe", "dve", "act", "pool", "sp")

    def __init__(self, nc, n_lanes=20):
        self.nc = nc
        self.lists = {e: [] for e in self.ENGS}
        self.state = {}
        self.inherit = {}
        self.keys_by_name = {}
        self.n_lanes = n_lanes
        self.phase = None
        self.scopes = False
        self.lane_rr = {e: 0 for e in self.ENGS}
        self.lane_last = {}
        self.lane_cnt = {}
        self.n = 0

    def _st(self, k):
        st = self.state.get(k)
        if st is None:
            rd = {}
            if isinstance(k, tuple):
                inh = self.inherit.get(k[0])
                if inh:
                    rd = dict(inh)
                self.keys_by_name.setdefault(k[0], []).append(k)
            st = self.state[k] = [None, rd]
        return st

    def users_of(self, name):
        out = {}
        for k in self.keys_by_name.get(name, ()):
            st = self.state[k]
            cands = list(st[1].values())
            if st[0] is not None:
                cands.append(st[0])
            for ins in cands:
                rk = ("d", ins.idx) if ins.is_dma else ins.eng
                o = out.get(rk)
                if o is None or o.idx < ins.idx:
                    out[rk] = ins
        return out

    def _add(self, eng, fn, reads, writes, is_dma):
        ins = Instr(eng, fn, is_dma)
        ins.phase = self.phase
        ins.idx = self.n
        self.n += 1
        deps = []
        for k in reads:
            st = self._st(k)
            if st[0] is not None:
                deps.append((st[0], "raw"))
            if isinstance(k, tuple) and k[0] == "ps":
                for r in st[1].values():
                    if r.eng != eng:
                        deps.append((r, "x"))
        for k in writes:
            st = self._st(k)
            if st[0] is not None:
                deps.append((st[0], "waw"))
            for r in st[1].values():
                deps.append((r, "war"))
        rk = ("d", ins.idx) if is_dma else eng
        for k in reads:
            self.state[k][1][rk] = ins
        for k in writes:
            st = self.state[k]
            st[0] = ins
            st[1] = {}
        seen = set()
        for p, kind in deps:
            if p is ins or id(p) in seen:
                continue
            if (not p.is_dma) and p.eng == eng and not is_dma:
                if eng == "pe":
                    continue
                if not SAME_ENGINE_RAW:
                    continue
            seen.add(id(p))
            ins.deps.append(p)
            p.sig = True
        if is_dma:
            key = (eng, self.lane_rr[eng] % self.n_lanes)
            self.lane_rr[eng] += 1
            ins.lane = key
            prev = self.lane_last.get(key)
            if prev is not None:
                ins.deps.append(prev)
            self.lane_last[key] = ins
            self.lane_cnt[key] = self.lane_cnt.get(key, 0) + 16
            ins.count = self.lane_cnt[key]
            ins.sig = True
        self.lists[eng].append(ins)
        return ins

    def op(self, eng, fn, reads=(), writes=()):
        return self._add(eng, fn, tuple(reads), tuple(writes), False)

    def dma(self, out, in_, reads=(), writes=(), eng="sp", **kw):
        fn = lambda e: e.dma_start(out=out, in_=in_, **kw)
        return self._add(eng, fn, tuple(reads), tuple(writes), True)

    def mm(self, out, lhsT, rhs, start, stop, reads, writes, **kw):
        return self.op("pe", lambda e: e.matmul(out, lhsT, rhs, start=start, stop=stop, **kw), reads, writes)

    def tr(self, out, in_, ident, reads, writes):
        return self.op("pe", lambda e: e.transpose(out, in_, ident), reads, writes)

    def act(self, out, in_, func, reads, writes, **kw):
        return self.op("act", lambda e: e.activation(out=out, in_=in_, func=func, **kw), reads, writes)

    def tt(self, eng, out, in0, in1, op, reads, writes):
        return self.op(eng, lambda e: e.tensor_tensor(out=out, in0=in0, in1=in1, op=op), reads, writes)

    def ts(self, eng, out, in0, s1, s2, op0, op1, reads, writes):
        if op1 is None:
            return self.op(eng, lambda e: e.tensor_scalar(out=out, in0=in0, scalar1=s1, scalar2=None, op0=op0), reads, writes)
        return self.op(eng, lambda e: e.tensor_scalar(out=out, in0=in0, scalar1=s1, scalar2=s2, op0=op0, op1=op1), reads, writes)

    def stt(self, out, in0, scalar, in1, op0, op1, reads, writes):
        return self.op("dve", lambda e: e.scalar_tensor_tensor(out=out, in0=in0, scalar=scalar, in1=in1, op0=op0, op1=op1), reads, writes)

    def cp(self, eng, out, in_, reads, writes):
        if eng == "act":
            return self.op("act", lambda e: e.copy(out=out, in_=in_), reads, writes)
        return self.op(eng, lambda e: e.tensor_copy(out=out, in_=in_), reads, writes)

    def emit(self):
        nc = self.nc
        with ExitStack() as es:
            esem = {}
            for e in ("pe", "dve", "act", "pool"):
                esem[e] = es.enter_context(nc.semaphore("s_" + e))
            lsem = {}
            for key in self.lane_cnt:
                lsem[key] = es.enter_context(nc.semaphore("l_%s_%d" % key))
            for e in ("pe", "dve", "act", "pool"):
                c = 0
                for ins in self.lists[e]:
                    if (not ins.is_dma) and ins.sig:
                        c += 1
                        ins.count = c
            block = es.enter_context(nc.Block())

            def replay(ename, eh):
                known = {}
                cur = None
                for ins in self.lists[ename]:
                    if self.scopes and ins.phase != cur:
                        if cur is not None:
                            nc.leave_named_scope(cur, sid, False)
                        cur = ins.phase
                        if cur is not None:
                            sid, _ = nc.enter_named_scope(cur, False)
                    for p in ins.deps:
                        if p.is_dma:
                            s, sk = lsem[p.lane], p.lane
                        else:
                            s, sk = esem[p.eng], p.eng
                        if known.get(sk, 0) >= p.count:
                            continue
                        known[sk] = p.count
                        eh.wait_ge(s, p.count)
                    r = ins.fn(eh)
                    if ins.is_dma:
                        r.then_inc(lsem[ins.lane], 16)
                    elif ins.sig:
                        r.then_inc(esem[ename], 1)
                if self.scopes and cur is not None:
                    nc.leave_named_scope(cur, sid, False)
                for key, last in self.lane_last.items():
                    if key[0] == ename and known.get(key, 0) < last.count:
                        eh.wait_ge(lsem[key], last.count)

            @block.tensor
            def _(e):
                replay("pe", e)

            @block.vector
            def _(e):
                replay("dve", e)

            @block.scalar
            def _(e):
                replay("act", e)

            @block.gpsimd
            def _(e):
                replay("pool", e)

            @block.sync
            def _(e):
                replay("sp", e)


class Buf:
    def __init__(self, name, ap):
        self.name = name
        self.ap = ap

    def __getitem__(self, idx):
        return self.ap[idx]

    def k(self, *i):
        return (self.name,) + tuple(i)


class Arena:
    def __init__(self, P, base_ap, nbytes):
        self.P = P
        self.base = base_ap
        self.nbytes = nbytes
        self.top = 0
        self.live = []
        self.dead = []
        self.gen = 0
        self.peak = 0

    def alloc(self, name, shape, dtype):
        esz = 4 if dtype == F32 else 2
        n = 1
        for s in shape:
            n *= s
        nb = (n * esz + 63) // 64 * 64
        start = self.top
        end = start + nb
        assert end <= self.nbytes, "arena overflow %s need %d have %d" % (name, end, self.nbytes)
        self.top = end
        self.peak = max(self.peak, end)
        self.gen += 1
        uname = "%s#%d" % (name, self.gen)
        inh = {}
        for (s0, e0, nm) in self.dead:
            if s0 < end and start < e0:
                for rk, ins in self.P.users_of(nm).items():
                    o = inh.get(rk)
                    if o is None or o.idx < ins.idx:
                        inh[rk] = ins
        if inh:
            self.P.inherit[uname] = inh
        ap = self.base[:, start // 2: start // 2 + n * esz // 2]
        if dtype == F32:
            ap = ap.bitcast(F32)
        if len(shape) > 1:
            names = " ".join("d%d" % i for i in range(len(shape)))
            kw = {"d%d" % i: shape[i] for i in range(len(shape))}
            ap = ap.rearrange("p (%s) -> p %s" % (names, names), **kw)
        self.live.append((start, end, uname))
        return Buf(uname, ap)

    def mark(self):
        return (self.top, len(self.live))

    def release(self, m):
        top, nl = m
        while len(self.live) > nl:
            self.dead.append(self.live.pop())
        self.top = top


def build_program(n_layers=DEPTH, debug=None):
    nc = bass.Bass("TRN2", target_bir_lowering=False, dynamic_dma_scratch_size=DGE_SCRATCH)
    dt = lambda name, shape, kind="ExternalInput": nc.dram_tensor(name, shape, F32, kind=kind).ap()
    x_d = dt("x", [T, D])
    ctx_d = dt("ctx", [TC, D])
    cc_d = dt("ccT", [128, 8, 2])
    wmod_d = dt("w_mod", [DEPTH, D, 6 * D])
    bmod_d = dt("bmodT", [128, DEPTH, 48])
    win_d = dt("w_in", [DEPTH, D, DIN])
    wout_d = dt("w_out", [DEPTH, D, D])
    wup_d = dt("w_up", [DEPTH, D, 2 * DFF])
    wdn_d = dt("w_down", [DEPTH, DFF, D])
    pfm_d = dt("pfm", [128, DEPTH, NPF])
    pbc_d = dt("pbc", [128, DEPTH, NPB])
    poolw_d = dt("poolw", [DEPTH, 128, 2, 128])
    fg_d = dt("final_g_bc", [128, D])
    ident_d = dt("ident", [128, 128])
    ropeA_d = dt("ropeA", [128, 16, 2, 16])
    ropeB_d = dt("ropeB", [128, 16, 2, 32])
    rel_d = dt("rel", [128, 4, 128])
    iot_d = dt("iot", [128, 2, 128])
    pcol_d = dt("pcol", [128, 2])
    pedge_d = dt("pooledge", [128, 2, 2, 8])
    out_d = dt("out", [T, D], kind="ExternalOutput")
    dbg_d = None
    if debug is not None:
        dbg_d = dt("dbg", list(debug), kind="ExternalOutput")

    es = ExitStack()
    ARENA_BYTES = (219 - (DGE_SCRATCH - 4096) // 1024) * 1024
    big = es.enter_context(nc.sbuf_tensor("arena", [128, ARENA_BYTES // 2], BF16))
    ps = es.enter_context(nc.psum_tensor("ps", [128, 8, 512], F32))
    P = Prog(nc)
    P.scopes = SCOPES[0]
    A = Arena(P, big[:], ARENA_BYTES)

    def PSK(b):
        return ("ps", b)

    def psb(b):
        return ps[:, b, :].bitcast(BF16)

    res = A.alloc("res", [NTILE, D], F32)
    hT = A.alloc("hT", [8, NT], BF16)
    ident_f = A.alloc("ident_f", [128], F32)
    ident_b = A.alloc("ident_b", [128], BF16)
    ones_f = A.alloc("ones_f", [128], F32)
    ropeA = A.alloc("ropeA", [16, 2, 16], F32)
    ropeB = A.alloc("ropeB", [16, 2, 32], F32)
    rel = A.alloc("rel", [4, 128], F32)
    iot = A.alloc("iot", [2, 128], F32)
    pcol = A.alloc("pcol", [2], F32)
    pedge = A.alloc("pedge", [2, 2, 8], F32)
    pfm = A.alloc("pfm", [DEPTH, NPF], F32)
    pbc = A.alloc("pbc", [DEPTH, NPB], F32)
    bmodT = A.alloc("bmodT", [DEPTH, 48], F32)
    scT = A.alloc("scT", [8, 2], F32)
    epsc = A.alloc("epsc", [1], F32)
    modT = A.alloc("modT", [48, 2], F32)
    A1 = A.alloc("A1", [8, 2], F32)
    A2 = A.alloc("A2", [8, 2], F32)
    gbc = A.alloc("gbc", [2, D], F32)
    ss = A.alloc("ss", [NTILE], F32)
    rstd = A.alloc("rstd", [NTILE], F32)
    lg = A.alloc("lg", [8], F32)
    lgsel = A.alloc("lgsel", [2, 2], F32)
    g128 = A.alloc("g128", [2, 2], F32)
    kdec = A.alloc("kdec", [2, 4], F32)
    DTt = A.alloc("DTt", [4, 128], F32)
    TFB = A.alloc("TFB", [2, 2, 128], F32)
    lamt = A.alloc("lamt", [4], F32)
    sublnS = A.alloc("sublnS", [64], F32)

    xv = x_d.rearrange("(i p) d -> p i d", p=128)
    for q in range(4):
        P.dma(res[:, 2 + 4 * q: 6 + 4 * q, :], xv[:, 4 * q: 4 * q + 4, :],
              writes=[res.k(2 + 4 * q + j) for j in range(4)])
    P.dma(res[:, 0:2, :], ctx_d.rearrange("(i p) d -> p i d", p=128), writes=[res.k(0), res.k(1)])
    for buf, src in ((ident_f, ident_d), (ropeA, ropeA_d), (ropeB, ropeB_d), (rel, rel_d), (iot, iot_d),
                     (pcol, pcol_d), (pedge, pedge_d), (pfm, pfm_d), (pbc, pbc_d), (bmodT, bmod_d), (scT, cc_d)):
        P.dma(buf.ap, src, writes=[buf.k()])
    P.op("dve", lambda e: e.memset(ones_f.ap, 1.0), writes=[ones_f.k()])
    P.op("dve", lambda e: e.memset(epsc.ap, EPS), writes=[epsc.k()])
    P.cp("dve", ident_b.ap, ident_f.ap, [ident_f.k()], [ident_b.k()])
    P.act(scT.ap, scT.ap, AF.Silu, [scT.k()], [scT.k()])

    def wcast(dst_ap, src_ap, writes):
        return P.dma(dst_ap, src_ap, writes=writes, eng="pool")

    def layer(l):
        need_ctx = l < DEPTH - 1
        lambda_init = 0.8 - 0.6 * math.exp(-0.3 * l)
        pf = lambda c0, c1: pfm[:, l, c0:c1]
        n1g, n2g = pf(0, 8), pf(8, 16)
        cdw_w = pf(16, 78).rearrange("p (c k) -> p c k", c=2)
        cdw_b, cln_g, cln_b, pscale = pf(78, 80), pf(80, 82), pf(82, 84), pf(84, 86)
        fdw_w = pf(86, 218).rearrange("p (j k) -> p j k", k=3)
        fdw_b = pf(218, 262)
        PFK = [pfm.k()]
        PBK = [pbc.k()]
        tiles_all = list(range(NTILE))
        tiles_x = list(range(2, NTILE))
        out_tiles = tiles_all if need_ctx else tiles_x

        P.phase = "L%d_mod" % l
        m0 = A.mark()
        wmv = wmod_d[l].rearrange("(k p) n -> p k n", p=128)
        wm = [A.alloc("wm%d" % i, [8, 512], F32) for i in range(2)]
        for pc in range(12):
            w = wm[pc % 2]
            P.dma(w.ap, wmv[:, :, pc * 512:(pc + 1) * 512], writes=[w.k()])
            for jj in range(4):
                col = (pc * 4 + jj) * 2
                for k in range(8):
                    P.mm(ps[:, 0, col:col + 2], w[:, k, jj * 128:(jj + 1) * 128], scT[:, k, :],
                         k == 0, k == 7, [w.k(), scT.k()], [PSK(0)])
        P.tt("dve", modT.ap, ps[:, 0, 0:96].rearrange("p (c w) -> p c w", w=2),
             bmodT[:, l, :].unsqueeze(2).to_broadcast([128, 48, 2]), ALU.add,
             [PSK(0), bmodT.k()], [modT.k()])
        for (Ax, ng, c0) in ((A1, n1g, 8), (A2, n2g, 32)):
            P.ts("dve", Ax.ap, modT[:, c0:c0 + 8, :], 1.0, None, ALU.add, None, [modT.k()], [Ax.k()])
            P.tt("dve", Ax.ap, Ax.ap, ng.unsqueeze(2).to_broadcast([128, 8, 2]), ALU.mult,
                 [Ax.k()] + PFK, [Ax.k()])
        A.release(m0)

        def make_gbc(c0):
            mg = A.mark()
            dg = A.alloc("dg", [8, 128], F32)
            for which in range(2):
                if which == 1 and not need_ctx:
                    continue
                for j in range(8):
                    P.ts("dve", dg[:, j, :], ident_f.ap, modT[:, c0 + j, which:which + 1], None, ALU.mult, None,
                         [ident_f.k(), modT.k()], [dg.k()])
                for half in range(2):
                    b = 1 + half
                    P.mm(ps[:, b, :], ones_f.ap, dg[:, 4 * half:4 * half + 4, :].rearrange("p a b -> p (a b)"),
                         True, True, [ones_f.k(), dg.k()], [PSK(b)])
                    P.cp("act", gbc[:, which, half * 512:(half + 1) * 512], ps[:, b, :],
                         [PSK(b)], [gbc.k(which)])
            A.release(mg)

        m0 = A.mark()
        A.release(m0)

        if STOP[0] == "mod":
            return
        P.phase = "L%d_tables" % l
        m0 = A.mark()
        tmpl = A.alloc("tmpl", [2, 32], F32)
        dl = pbc[:, l, 64:192].rearrange("p (a b c) -> p a b c", a=2, b=2)
        P.tt("dve", tmpl.ap, dl[:, :, 0, :], dl[:, :, 1, :], ALU.mult, PBK, [tmpl.k()])
        P.op("dve", lambda e: e.tensor_reduce(out=lamt[:, 0:2], in_=tmpl.ap, axis=AX.X, op=ALU.add), [tmpl.k()], [lamt.k()])
        P.act(lamt[:, 0:2], lamt[:, 0:2], AF.Exp, [lamt.k()], [lamt.k()])
        P.tt("dve", lamt[:, 2:3], lamt[:, 1:2], lamt[:, 0:1], ALU.subtract, [lamt.k()], [lamt.k()])
        P.ts("dve", lamt[:, 3:4], lamt[:, 2:3], -lambda_init, None, ALU.add, None, [lamt.k()], [lamt.k()])
        P.ts("dve", sublnS.ap, pbc[:, l, 0:64], 1.0 - lambda_init, None, ALU.mult, None, PBK, [sublnS.k()])
        P.act(lg.ap, pbc[:, l, 192:200], AF.Exp, PBK, [lg.k()])
        P.ts("dve", lg.ap, lg.ap, -1.0, None, ALU.mult, None, [lg.k()], [lg.k()])
        for hp in range(2):
            for d_ in range(2):
                for half in range(2):
                    sl = slice(64 * half, 64 * half + 64)
                    P.cp("dve", lgsel[sl, hp, d_:d_ + 1], lg[sl, d_ * 4 + 2 * hp + half: d_ * 4 + 2 * hp + half + 1],
                         [lg.k()], [lgsel.k()])
        P.act(g128.ap, lgsel.ap, AF.Exp, [lgsel.k()], [g128.k()], scale=128.0)
        for d_ in range(2):
            P.act(kdec[:, d_, :], lg[:, d_ * 4:d_ * 4 + 4], AF.Exp, [lg.k(), pcol.k()], [kdec.k()], scale=pcol[:, d_:d_ + 1])
        P.ts("dve", kdec.ap, kdec.ap, 0.125, None, ALU.mult, None, [kdec.k()], [kdec.k()])
        tmpd = A.alloc("tmpd", [2, 128], F32)
        for h in range(4):
            P.act(tmpd[:, 0, :], rel[:, 0, :], AF.Exp, [rel.k(), lg.k()], [tmpd.k()], scale=lg[:, h:h + 1])
            P.act(tmpd[:, 1, :], rel[:, 1, :], AF.Exp, [rel.k(), lg.k()], [tmpd.k()], scale=lg[:, 4 + h:5 + h])
            P.tt("dve", tmpd.ap, tmpd.ap, rel[:, 2:4, :], ALU.mult, [tmpd.k(), rel.k()], [tmpd.k()])
            P.tt("dve", DTt[:, h, :], tmpd[:, 0, :], tmpd[:, 1, :], ALU.add, [tmpd.k()], [DTt.k()])
        P.ts("dve", DTt.ap, DTt.ap, 0.125, None, ALU.mult, None, [DTt.k()], [DTt.k()])
        for hp in range(2):
            for d_ in range(2):
                P.act(TFB[:, hp, d_, :], iot[:, d_, :], AF.Exp, [iot.k(), lgsel.k()], [TFB.k()],
                      scale=lgsel[:, hp, d_:d_ + 1])
        A.release(m0)

        if STOP[0] == "tables":
            return
        def norm_steps(Ax, bcol, tiles, xn, bank0):
            steps = []
            for gi_, g0 in enumerate(range(0, len(tiles), 2)):
                def step(gi_=gi_, g0=g0):
                    grp = tiles[g0:g0 + 2]
                    xb = xn[gi_ % 2]
                    b0 = bank0 + 2 * (gi_ % 2)
                    for s, i in enumerate(grp):
                        P.act(xb[:, s, :], res[:, i, :], AF.Square, [res.k(i)], [xb.k(s), ss.k(i)], accum_out=ss[:, i:i + 1])
                        P.act(rstd[:, i:i + 1], ss[:, i:i + 1], AF.Sqrt, [ss.k(i), epsc.k()], [rstd.k(i)],
                              scale=1.0 / D, bias=epsc[:, 0:1])
                        P.op("dve", lambda e, i=i: e.reciprocal(out=rstd[:, i:i + 1], in_=rstd[:, i:i + 1]),
                             [rstd.k(i)], [rstd.k(i)])
                        P.ts("dve", xb[:, s, :], res[:, i, :], rstd[:, i:i + 1], None, ALU.mult, None,
                             [res.k(i), rstd.k(i)], [xb.k(s)])
                        for j in range(8):
                            bank = b0 + j // 4
                            o = psb(bank)[:, (j % 4) * 256 + s * 128:(j % 4) * 256 + s * 128 + 128]
                            P.tr(o, xb[:, s, j * 128:(j + 1) * 128], ident_b.ap, [xb.k(s), ident_b.k()], [PSK(bank)])
                    which = 1 if grp[0] < 2 else 0
                    t0 = grp[0] * 128
                    for j in range(8):
                        bank = b0 + j // 4
                        src = psb(bank)[:, (j % 4) * 256:(j % 4) * 256 + 256]
                        dst = hT[:, j, t0:t0 + 256]
                        rk = [PSK(bank), Ax.k(), modT.k()]
                        wk = [hT.k(grp[0]), hT.k(grp[1])]
                        if j % 2 == 0:
                            P.ts("dve", dst, src, Ax[:, j, which:which + 1], modT[:, bcol + j, which:which + 1],
                                 ALU.mult, ALU.add, rk, wk)
                        else:
                            P.act(dst, src, AF.Identity, rk, wk, scale=Ax[:, j, which:which + 1],
                                  bias=modT[:, bcol + j, which:which + 1])
                steps.append(step)
            return steps

        def norm_mod(Ax, bcol, tiles):
            m = A.mark()
            xn = [A.alloc("xn%d" % i, [2, D], BF16) for i in range(2)]
            for st in norm_steps(Ax, bcol, tiles, xn, 0):
                st()
            A.release(m)

        norm_mod(A1, 0, tiles_all)

        if STOP[0] == "norm1":
            return
        win_v = win_d[l].rearrange("(k p) n -> p k n", p=128)
        mL = A.mark()
        ycs = [None] * 4
        ycs[0] = A.alloc("ycA", [2, NT], BF16)

        P.phase = "L%d_Aproj" % l
        mA = A.mark()
        qkT = A.alloc("qkT", [4, NT], BF16)
        vaug = A.alloc("vaug", [NTILE, 4, 65], BF16)
        P.op("pool", lambda e: e.memset(vaug.ap, 1.0), [], [vaug.k(i) for i in range(NTILE)])
        mA2 = A.mark()
        WA = A.alloc("WA", [8, 768], BF16)
        wcast(WA[:, :, 0:256], win_v[:, :, 0:256], [WA.k()])
        wcast(WA[:, :, 256:768], win_v[:, :, 1536:2048], [WA.k()])
        stg = [A.alloc("stgA%d" % i, [512], BF16) for i in range(2)]
        rt = [A.alloc("rtA%d" % i, [256], F32) for i in range(4)]
        ptiles = tiles_all
        for n_, i in enumerate(ptiles):
            bA, bB = (0, 1) if n_ % 2 == 0 else (2, 3)
            tok = slice(i * 128, (i + 1) * 128)
            need_q = (i >= 2) or need_ctx
            c0 = 0 if need_q else 256
            for k in range(8):
                P.mm(ps[:, bA, c0:512], hT[:, k, tok], WA[:, k, c0:512], k == 0, k == 7, [hT.k(i), WA.k()], [PSK(bA)])
            for k in range(8):
                P.mm(ps[:, bB, 0:256], hT[:, k, tok], WA[:, k, 512:768], k == 0, k == 7, [hT.k(i), WA.k()], [PSK(bB)])
            sg = stg[n_ % 2]
            if i >= 2:
                xi = i - 2
                v = ps[:, bA, :].rearrange("p (g i j r) -> p g i j r", g=16, i=2, j=2)
                a_, b_ = v[:, :, :, 0, :], v[:, :, :, 1, :]
                sv = sg.ap.rearrange("p (g i j r) -> p g i j r", g=16, i=2, j=2)
                cs = ropeA[:, xi, 0, :].rearrange("p (i r) -> p i r", i=2).unsqueeze(1).to_broadcast([128, 16, 2, 8])
                sn = ropeA[:, xi, 1, :].rearrange("p (i r) -> p i r", i=2).unsqueeze(1).to_broadcast([128, 16, 2, 8])
                r4 = [r_.ap.rearrange("p (g i r) -> p g i r", g=16, i=2) for r_ in rt]
                RK = [PSK(bA), ropeA.k()]
                P.tt("dve", r4[0], a_, cs, ALU.mult, RK, [rt[0].k()])
                P.tt("dve", r4[1], b_, sn, ALU.mult, RK, [rt[1].k()])
                P.tt("dve", r4[2], a_, sn, ALU.mult, RK, [rt[2].k()])
                P.tt("dve", r4[3], b_, cs, ALU.mult, RK, [rt[3].k()])
                P.tt("pool", sv[:, :, :, 0, :], r4[0], r4[1], ALU.subtract, [rt[0].k(), rt[1].k()], [sg.k()])
                P.tt("pool", sv[:, :, :, 1, :], r4[2], r4[3], ALU.add, [rt[2].k(), rt[3].k()], [sg.k()])
            else:
                P.cp("act", sg[:, c0:512], ps[:, bA, c0:512], [PSK(bA)], [sg.k()])
            P.cp("act", vaug[:, i, :, 0:64], ps[:, bB, 0:256].rearrange("p (h d) -> p h d", h=4), [PSK(bB)], [vaug.k(i)])
            bT = 4 + n_ % 2
            cl = list(range(4)) if need_q else [2, 3]
            for c in cl:
                P.tr(psb(bT)[:, c * 128:(c + 1) * 128], sg[:, c * 128:(c + 1) * 128], ident_b.ap,
                     [sg.k(), ident_b.k()], [PSK(bT)])
            P.cp("act", qkT[:, cl[0]:4, tok], psb(bT)[:, cl[0] * 128:512].rearrange("p (c t) -> p c t", t=128),
                 [PSK(bT)], [qkT.k(i)])
        A.release(mA2)

        if STOP[0] == "A_proj":
            A.release(mL)
            return
        P.phase = "L%d_Amain" % l
        PT = [A.alloc("PT%d" % i, [512], BF16) for i in range(3)]
        oA = A.alloc("oA", [4, 256], F32)
        rcp = A.alloc("rcp", [2, 4], F32)
        otmp = A.alloc("otmp", [64], F32)
        aT = [A.alloc("aT%d" % i, [512], F32) for i in range(2)]
        sqA = A.alloc("sqA", [256], F32)
        ssA = A.alloc("ssA", [4], F32)
        yA = A.alloc("yA", [256], BF16)
        sc_exp = 32.0 ** -0.5
        acc_ctr = [0]

        def attn_block(q0, nq, ktiles):
            nqb = nq // 128
            qtile0 = q0 // 128
            for h in range(4):
                ab = (3, 4) if acc_ctr[0] % 2 == 0 else (5, 6)
                acc_ctr[0] += 1
                steps = [(kt, m) for kt in ktiles for m in range(2)]

                def qk(sidx):
                    kt, m = steps[sidx]
                    rb = (h % 2) * 64 + m * 32
                    bS = sidx % 3
                    P.mm(ps[:, bS, 0:nq], qkT[rb:rb + 32, 2 + h // 2, kt * 128:(kt + 1) * 128],
                         qkT[rb:rb + 32, h // 2, q0:q0 + nq], True, True,
                         [qkT.k(kt)] + [qkT.k(qtile0 + j) for j in range(nqb)], [PSK(bS)], tile_position=(rb, 0))
                    P.act(PT[bS][:, 0:nq], ps[:, bS, 0:nq], AF.Exp, [PSK(bS)], [PT[bS].k()], scale=sc_exp)

                def pv(sidx):
                    kt, m = steps[sidx]
                    bS = sidx % 3
                    P.mm(ps[0:65, ab[m], 0:nq], vaug[:, kt, h, :], PT[bS][:, 0:nq], kt == ktiles[0], kt == ktiles[-1],
                         [PT[bS].k(), vaug.k(kt)], [PSK(ab[m])])

                for s_ in range(len(steps) + 2):
                    if s_ < len(steps):
                        qk(s_)
                    if s_ >= 2:
                        pv(s_ - 2)
                for m in range(2):
                    P.cp("act", aT[m][0:65, 0:nq], ps[0:65, ab[m], 0:nq], [PSK(ab[m])], [aT[m].k()])
                for bi in range((nqb + 1) // 2):
                    bank = 7
                    for qq in range(2):
                        qb = bi * 2 + qq
                        if qb >= nqb:
                            continue
                        for m in range(2):
                            col = (qq * 2 + m) * 65
                            P.tr(ps[:, bank, col:col + 65], aT[m][0:65, qb * 128:(qb + 1) * 128], ident_f[0:65, 0:65],
                                 [aT[m].k(), ident_f.k()], [PSK(bank)])
                    av = ps[:, bank, 0:260].rearrange("p (g c) -> p g c", c=65)
                    P.op("dve", lambda e, av=av, bi=bi: e.reciprocal(out=rcp[:, bi, :], in_=av[:, :, 64]),
                         [PSK(bank)], [rcp.k()])
                    rv = rcp[:, bi, :].rearrange("p (q m) -> p q m", m=2)
                    P.ts("dve", rv[:, :, 1], rv[:, :, 1], lamt[:, 3:4], None, ALU.mult, None, [rcp.k(), lamt.k()], [rcp.k()])
                    for qq in range(2):
                        qb = bi * 2 + qq
                        if qb >= nqb:
                            continue
                        P.ts("dve", otmp.ap, av[:, qq * 2, 0:64], rcp[:, bi, qq * 2:qq * 2 + 1], None, ALU.mult, None,
                             [PSK(bank), rcp.k()], [otmp.k()])
                        P.stt(oA[:, qb, h * 64:(h + 1) * 64], av[:, qq * 2 + 1, 0:64], rcp[:, bi, qq * 2 + 1:qq * 2 + 2],
                              otmp.ap, ALU.mult, ALU.add, [PSK(bank), rcp.k(), otmp.k()], [oA.k(qb)])
            for qb in range(nqb):
                ti = qtile0 + qb
                o4 = oA[:, qb, :].rearrange("p (h d) -> p h d", h=4)
                P.tt("pool", sqA.ap, oA[:, qb, :], oA[:, qb, :], ALU.mult, [oA.k(qb)], [sqA.k()])
                P.op("dve", lambda e: e.tensor_reduce(out=ssA.ap, in_=sqA.ap.rearrange("p (h d) -> p h d", h=4), axis=AX.X, op=ALU.add),
                     [sqA.k()], [ssA.k()])
                P.act(ssA.ap, ssA.ap, AF.Sqrt, [ssA.k(), epsc.k()], [ssA.k()], scale=1.0 / 64, bias=epsc[:, 0:1])
                P.op("dve", lambda e: e.reciprocal(out=ssA.ap, in_=ssA.ap), [ssA.k()], [ssA.k()])
                s4 = sqA.ap.rearrange("p (h d) -> p h d", h=4)
                P.tt("dve", s4, o4, ssA.ap.unsqueeze(2).to_broadcast([128, 4, 64]), ALU.mult, [oA.k(qb), ssA.k()], [sqA.k()])
                P.tt("dve", yA.ap.rearrange("p (h d) -> p h d", h=4), s4, sublnS.ap.unsqueeze(1).to_broadcast([128, 4, 64]),
                     ALU.mult, [sqA.k(), sublnS.k()], [yA.k()])
                for c in range(2):
                    P.tr(psb(7)[:, c * 128:(c + 1) * 128], yA[:, c * 128:(c + 1) * 128], ident_b.ap, [yA.k(), ident_b.k()], [PSK(7)])
                P.cp("act", ycs[0][:, 0:2, ti * 128:(ti + 1) * 128], psb(7)[:, 0:256].rearrange("p (c t) -> p c t", t=128),
                     [PSK(7)], [ycs[0].k(0, ti), ycs[0].k(1, ti)])

        if need_ctx:
            attn_block(0, 256, [0, 1])
        for qt in range(4):
            attn_block(256 + qt * 512, 512, tiles_all)
        A.release(mA)

        if STOP[0] == "A":
            A.release(mL)
            return
        P.phase = "L%d_B" % l
        ycs[1] = A.alloc("ycB", [2, NT], BF16)
        for hp in range(2):
            mB = A.mark()
            qT = A.alloc("qT", [NT], BF16)
            kT = A.alloc("kT", [NT], BF16)
            vt = A.alloc("vt", [NTILE, 128], BF16)
            Sall = A.alloc("Sall", [NTILE, 2, 64], F32)
            SallB = A.alloc("SallB", [NTILE, 2, 64], BF16)
            WB = A.alloc("WB", [8, 512], BF16)
            mB2 = A.mark()
            for ci, cb in enumerate((256, 2048, 2304, 512)):
                wcast(WB[:, :, ci * 128:(ci + 1) * 128], win_v[:, :, cb + hp * 128: cb + hp * 128 + 128], [WB.k()])
            fo = list(range(NTILE))
            bo = [1, 0] + list(range(NTILE - 1, 1, -1))
            nxt_of = [{fo[a]: fo[a + 1] for a in range(NTILE - 1)}, {bo[a]: bo[a + 1] for a in range(NTILE - 1)}]
            stg = [A.alloc("stgB%d" % i, [256], BF16) for i in range(2)]
            rt = [A.alloc("rtB%d" % i, [128], F32) for i in range(4)]
            kfb = [A.alloc("kfb%d" % i, [2, 128], BF16) for i in range(2)]
            for n_, i in enumerate(tiles_all):
                bP = n_ % 2
                tok = slice(i * 128, (i + 1) * 128)
                for k in range(8):
                    P.mm(ps[:, bP, 0:384], hT[:, k, tok], WB[:, k, 0:384], k == 0, k == 7, [hT.k(i), WB.k()], [PSK(bP)])
                sg = stg[n_ % 2]
                if i >= 2:
                    xi = i - 2
                    v = ps[:, bP, 0:256].rearrange("p (g i j r) -> p g i j r", g=4, i=2, j=2)
                    a_, b_ = v[:, :, :, 0, :], v[:, :, :, 1, :]
                    sv = sg.ap.rearrange("p (g i j r) -> p g i j r", g=4, i=2, j=2)
                    cs = ropeB[:, xi, 0, :].rearrange("p (i r) -> p i r", i=2).unsqueeze(1).to_broadcast([128, 4, 2, 16])
                    sn = ropeB[:, xi, 1, :].rearrange("p (i r) -> p i r", i=2).unsqueeze(1).to_broadcast([128, 4, 2, 16])
                    r4 = [r_.ap.rearrange("p (g i r) -> p g i r", g=4, i=2) for r_ in rt]
                    RK = [PSK(bP), ropeB.k()]
                    P.tt("dve", r4[0], a_, cs, ALU.mult, RK, [rt[0].k()])
                    P.tt("dve", r4[1], b_, sn, ALU.mult, RK, [rt[1].k()])
                    P.tt("dve", r4[2], a_, sn, ALU.mult, RK, [rt[2].k()])
                    P.tt("dve", r4[3], b_, cs, ALU.mult, RK, [rt[3].k()])
                    P.tt("pool", sv[:, :, :, 0, :], r4[0], r4[1], ALU.subtract, [rt[0].k(), rt[1].k()], [sg.k()])
                    P.tt("pool", sv[:, :, :, 1, :], r4[2], r4[3], ALU.add, [rt[2].k(), rt[3].k()], [sg.k()])
                else:
                    P.cp("act", sg.ap, ps[:, bP, 0:256], [PSK(bP)], [sg.k()])
                if "vt" not in SKIP:
                    P.cp("dve", vt[:, i, :], ps[:, bP, 256:384], [PSK(bP)], [vt.k(i)])
                k3 = sg[:, 128:256].rearrange("p (h d) -> p h d", h=2)
                kd = kfb[n_ % 2]
                for d_ in (range(2) if "kd" not in SKIP else []):
                    P.tt("dve", kd[:, d_, :].rearrange("p (h d) -> p h d", h=2), k3,
                         kdec[:, d_, 2 * hp:2 * hp + 2].unsqueeze(2).to_broadcast([128, 2, 64]), ALU.mult,
                         [sg.k(), kdec.k()], [kd.k(d_)])
                bT = 2 + n_ % 2
                if "tr" not in SKIP:
                    for c in range(2):
                        P.tr(psb(bT)[:, c * 128:(c + 1) * 128], sg[:, c * 128:(c + 1) * 128], ident_b.ap,
                             [sg.k(), ident_b.k()], [PSK(bT)])
                    P.cp("act", qT[:, tok], psb(bT)[:, 0:128], [PSK(bT)], [qT.k(i)])
                    P.cp("act", kT[:, tok], psb(bT)[:, 128:256], [PSK(bT)], [kT.k(i)])
                if "U" in SKIP:
                    continue
                bU = 4 + n_ % 2
                P.mm(ps[:, bU, 0:128], kd[:, 0, :], vt[:, i, :], True, True, [kd.k(0), vt.k(i)], [PSK(bU)])
                P.mm(ps[:, bU, 128:256], kd[:, 1, :], vt[:, i, :], True, True, [kd.k(1), vt.k(i)], [PSK(bU)])
                uv = ps[:, bU, 0:256].rearrange("p (d n) -> p d n", d=2)
                for d_ in (range(2) if "Sc" not in SKIP else []):
                    nx = nxt_of[d_].get(i)
                    if nx is None:
                        continue
                    P.cp("dve", Sall[0:64, nx, d_, :], uv[0:64, d_, 0:64], [PSK(bU)], [Sall.k(nx, d_)])
                    P.cp("dve", Sall[64:128, nx, d_, :], uv[64:128, d_, 64:128], [PSK(bU)], [Sall.k(nx, d_)])
            if STOP[0] == "B1":
                A.release(mL)
                return
            P.op("dve", lambda e: e.memset(Sall[:, 0, 0, :], 0.0), [], [Sall.k(0, 0)])
            P.op("dve", lambda e: e.memset(Sall[:, 1, 1, :], 0.0), [], [Sall.k(1, 1)])
            for d_, order in ((0, fo), (1, bo)):
                for a_i in range(len(order) - 1):
                    cur, nxt = order[a_i], order[a_i + 1]
                    P.stt(Sall[:, nxt, d_, :], Sall[:, cur, d_, :], g128[:, hp, d_:d_ + 1], Sall[:, nxt, d_, :],
                          ALU.mult, ALU.add, [Sall.k(cur, d_), g128.k(), Sall.k(nxt, d_)], [Sall.k(nxt, d_)])
            P.cp("act", SallB.ap, Sall.ap, [Sall.k(i, d_) for i in range(NTILE) for d_ in range(2)], [SallB.k()])
            if STOP[0] == "B2":
                A.release(mL)
                return
            A.release(mB2)
            Pm = [A.alloc("Pm%d" % i, [256], BF16) for i in range(2)]
            qfb = [A.alloc("qfb%d" % i, [2, 128], BF16) for i in range(2)]
            sqB = A.alloc("sqB", [128], F32)
            ssB = A.alloc("ssB", [2], F32)
            yB = A.alloc("yB", [128], BF16)
            gsb = A.alloc("gsb", [128], F32)
            for n_, i in enumerate(out_tiles):
                tok = slice(i * 128, (i + 1) * 128)
                bA_ = n_ % 2
                bO = 2 + n_ % 2
                for h in range(2):
                    rs = slice(64 * h, 64 * h + 64)
                    P.mm(ps[:, h, 0:128], kT[rs, tok], qT[rs, tok], True, True,
                         [kT.k(i), qT.k(i)], [PSK(h)], tile_position=(64 * h, 0))
                pm = Pm[n_ % 2]
                qd = qfb[n_ % 2]
                for d_ in range(2):
                    P.tt("pool", qd[:, d_, :], qT[:, tok], TFB[:, hp, d_, :], ALU.mult, [qT.k(i), TFB.k()], [qd.k(d_)])
                for h in range(2):
                    P.tt("dve", pm[:, h * 128:(h + 1) * 128], ps[:, h, 0:128], DTt[:, 2 * hp + h, :], ALU.mult,
                         [PSK(h), DTt.k()], [pm.k()])
                for h in range(2):
                    rs = slice(64 * h, 64 * h + 64)
                    o = ps[:, bO, h * 64:(h + 1) * 64]
                    P.mm(o, pm[:, h * 128:(h + 1) * 128], vt[:, i, h * 64:(h + 1) * 64], True, False,
                         [pm.k(), vt.k(i)], [PSK(bO)])
                    P.mm(o, qd[rs, 0, :], SallB[rs, i, 0, :], False, False, [qd.k(0), SallB.k()], [PSK(bO)],
                         tile_position=(64 * h, 0))
                    P.mm(o, qd[rs, 1, :], SallB[rs, i, 1, :], False, True, [qd.k(1), SallB.k()], [PSK(bO)],
                         tile_position=(64 * h, 0))
                ov = ps[:, bO, 0:128]
                P.act(sqB.ap, ov, AF.Square, [PSK(bO)], [sqB.k()])
                P.op("dve", lambda e: e.tensor_reduce(out=ssB.ap, in_=sqB.ap.rearrange("p (h d) -> p h d", h=2), axis=AX.X, op=ALU.add),
                     [sqB.k()], [ssB.k()])
                P.act(ssB.ap, ssB.ap, AF.Sqrt, [ssB.k(), epsc.k()], [ssB.k()], scale=1.0 / 64, bias=epsc[:, 0:1])
                P.op("dve", lambda e: e.reciprocal(out=ssB.ap, in_=ssB.ap), [ssB.k()], [ssB.k()])
                s3 = sqB.ap.rearrange("p (h d) -> p h d", h=2)
                P.tt("dve", s3, ov.rearrange("p (h d) -> p h d", h=2), ssB.ap.unsqueeze(2).to_broadcast([128, 2, 64]),
                     ALU.mult, [PSK(bO), ssB.k()], [sqB.k()])
                bG = 4 + n_ % 2
                for k in range(8):
                    P.mm(ps[:, bG, 0:128], hT[:, k, tok], WB[:, k, 384:512], k == 0, k == 7, [hT.k(i), WB.k()], [PSK(bG)])
                P.act(gsb.ap, ps[:, bG, 0:128], AF.Silu, [PSK(bG)], [gsb.k()])
                P.tt("pool", yB.ap, sqB.ap, gsb.ap, ALU.mult, [sqB.k(), gsb.k()], [yB.k()])
                bT = 6 + n_ % 2
                P.tr(psb(bT)[:, 0:128], yB.ap, ident_b.ap, [yB.k(), ident_b.k()], [PSK(bT)])
                P.cp("act", ycs[1][:, hp, tok], psb(bT)[:, 0:128], [PSK(bT)], [ycs[1].k(hp, i)])
            A.release(mB)

        if STOP[0] == "B":
            A.release(mL)
            return
        tgs = ([(0, 256)] if need_ctx else []) + [(256 + 512 * q, 512) for q in range(4)]

        def tg_tiles(t0, n):
            return list(range(t0 // 128, (t0 + n) // 128))

        P.phase = "L%d_C" % l
        ycs[2] = A.alloc("ycC", [2, NT], BF16)
        mC = A.mark()
        PADC = 16
        LC = (TC + 2 * PADC) + (T + 2 * PADC)
        upad = A.alloc("upad", [2, LC], BF16)

        def cpos(t):
            return PADC + t if t < TC else (TC + 2 * PADC) + PADC + (t - TC)
        P.op("pool", lambda e: e.memset(upad.ap, 0.0), [], [upad.k(c, i) for c in range(2) for i in range(NTILE)])
        Dg = A.alloc("Dg", [2, 31, 128], BF16)
        for c in range(2):
            for k in range(31):
                P.ts("dve", Dg[:, c, k, :], ident_f.ap, cdw_w[:, c, k:k + 1], None, ALU.mult, None,
                     [ident_f.k()] + PFK, [Dg.k()])
        mC2 = A.mark()
        WC = A.alloc("WC", [8, 512], BF16)
        wcast(WC.ap, win_v[:, :, 768:1280], [WC.k()])
        sig = [A.alloc("sig%d" % i, [512], F32) for i in range(2)]
        for gi_, (t0, n) in enumerate(tgs):
            tl = tg_tiles(t0, n)
            for c in range(2):
                n_ = gi_ * 2 + c
                ba, bg = (0, 1) if n_ % 2 == 0 else (2, 3)
                for k in range(8):
                    P.mm(ps[:, ba, 0:n], WC[:, k, c * 128:(c + 1) * 128], hT[:, k, t0:t0 + n], k == 0, k == 7,
                         [WC.k()] + [hT.k(i) for i in tl], [PSK(ba)])
                for k in range(8):
                    P.mm(ps[:, bg, 0:n], WC[:, k, 256 + c * 128:256 + (c + 1) * 128], hT[:, k, t0:t0 + n], k == 0, k == 7,
                         [WC.k()] + [hT.k(i) for i in tl], [PSK(bg)])
                sgb = sig[n_ % 2]
                P.act(sgb[:, 0:n], ps[:, bg, 0:n], AF.Sigmoid, [PSK(bg)], [sgb.k()])
                p0 = cpos(t0)
                P.tt("dve", upad[:, c, p0:p0 + n], ps[:, ba, 0:n], sgb[:, 0:n], ALU.mult, [PSK(ba), sgb.k()],
                     [upad.k(c, i) for i in tl])
        A.release(mC2)
        y32 = A.alloc("y32", [2, 512], F32)
        ysq = A.alloc("ysq", [2, 512], F32)
        mean = A.alloc("mean", [512], F32)
        var = A.alloc("var", [512], F32)
        dtm = A.alloc("dtm", [512], F32)
        for gi_, (t0, n) in enumerate(tgs):
            tl = tg_tiles(t0, n)
            lo_t = max(tl[0] - 1, 0 if t0 < TC else 2)
            hi_t = min(tl[-1] + 1, 1 if t0 < TC else NTILE - 1)
            rtl = list(range(lo_t, hi_t + 1))
            p0 = cpos(t0)
            for c in range(2):
                bc_ = 4 + c
                for k in range(31):
                    P.mm(ps[:, bc_, 0:n], Dg[:, c, k, :], upad[:, c, p0 + k - 15:p0 + k - 15 + n], k == 0, k == 30,
                         [Dg.k()] + [upad.k(c, i) for i in rtl], [PSK(bc_)])
                P.act(y32[:, c, 0:n], ps[:, bc_, 0:n], AF.Identity, [PSK(bc_)] + PFK, [y32.k(c)], bias=cdw_b[:, c:c + 1])
                P.act(ysq[:, c, 0:n], ps[:, bc_, 0:n], AF.Square, [PSK(bc_)] + PFK, [ysq.k(c)], bias=cdw_b[:, c:c + 1])
            for c in range(2):
                P.mm(ps[:, 6, 0:n], ones_f.ap, y32[:, c, 0:n], c == 0, c == 1, [ones_f.k(), y32.k(c)], [PSK(6)])
            for c in range(2):
                P.mm(ps[:, 7, 0:n], ones_f.ap, ysq[:, c, 0:n], c == 0, c == 1, [ones_f.k(), ysq.k(c)], [PSK(7)])
            P.ts("dve", mean[:, 0:n], ps[:, 6, 0:n], 1.0 / 256, None, ALU.mult, None, [PSK(6)], [mean.k()])
            P.tt("dve", dtm[:, 0:n], mean[:, 0:n], mean[:, 0:n], ALU.mult, [mean.k()], [dtm.k()])
            P.stt(var[:, 0:n], ps[:, 7, 0:n], 1.0 / 256, dtm[:, 0:n], ALU.mult, ALU.subtract, [PSK(7), dtm.k()], [var.k()])
            P.act(var[:, 0:n], var[:, 0:n], AF.Ln, [var.k(), epsc.k()], [var.k()], bias=epsc[:, 0:1])
            P.act(var[:, 0:n], var[:, 0:n], AF.Exp, [var.k()], [var.k()], scale=-0.5)
            for c in range(2):
                P.tt("dve", dtm[:, 0:n], y32[:, c, 0:n], mean[:, 0:n], ALU.subtract, [y32.k(c), mean.k()], [dtm.k()])
                P.tt("dve", dtm[:, 0:n], dtm[:, 0:n], var[:, 0:n], ALU.mult, [dtm.k(), var.k()], [dtm.k()])
                P.act(ycs[2][:, c, t0:t0 + n], dtm[:, 0:n], AF.Silu, [dtm.k()] + PFK, [ycs[2].k(c, i) for i in tl],
                      scale=cln_g[:, c:c + 1], bias=cln_b[:, c:c + 1])
        A.release(mC)

        if STOP[0] == "C":
            A.release(mL)
            return
        P.phase = "L%d_D" % l
        ycs[3] = A.alloc("ycD", [2, NT], BF16)
        mD = A.mark()
        WD = A.alloc("WD", [8, 256], BF16)
        wcast(WD.ap, win_v[:, :, 1280:1536], [WD.k()])
        PWt = A.alloc("PWt", [2, 128], BF16)
        wcast(PWt.ap, poolw_d[l], [PWt.k()])
        PD = 16
        LD = (TC + 2 * PD) + (T + 2 * PD)
        seqs = ([(0, TC, 0)] if need_ctx else []) + [(TC, T, TC + 2 * PD)]
        pl = A.alloc("pl", [NT], BF16)
        for c in range(2):
            mDc = A.mark()
            ub = A.alloc("ub", [LD], F32)
            wb1 = A.alloc("wb1", [LD], F32)
            wb2 = A.alloc("wb2", [LD], F32)
            P.op("pool", lambda e, ub=ub: e.memset(ub.ap, 0.0), [], [ub.k()])
            for gi_, (t0, n) in enumerate(tgs):
                tl = tg_tiles(t0, n)
                bk = gi_ % 2
                for k in range(8):
                    P.mm(ps[:, bk, 0:n], WD[:, k, c * 128:(c + 1) * 128], hT[:, k, t0:t0 + n], k == 0, k == 7,
                         [WD.k()] + [hT.k(i) for i in tl], [PSK(bk)])
                base = PD + t0 if t0 < TC else (TC + 2 * PD) + PD + (t0 - TC)
                P.cp("act", ub[:, base:base + n], ps[:, bk, 0:n], [PSK(bk)], [ub.k()])
            for (tk0, n, pb) in seqs:
                z = pb + PD
                P.tt("dve", wb1[:, z - 8:z + n + 8], ub[:, z - 9:z + n + 7], ub[:, z - 8:z + n + 8], ALU.add, [ub.k()], [wb1.k()])
                P.tt("dve", wb2[:, z - 6:z + n + 6], wb1[:, z - 7:z + n + 5], wb1[:, z - 5:z + n + 7], ALU.add, [wb1.k()], [wb2.k()])
                finals = [wb1, wb2]
                wins = (2, 4)
                if c == 1:
                    P.tt("dve", wb1[:, z - 4:z + n + 4], wb2[:, z - 6:z + n + 2], wb2[:, z - 2:z + n + 6], ALU.add, [wb2.k()], [wb1.k()])
                    P.tt("dve", wb2[64:128, z:z + n], wb1[64:128, z - 4:z + n - 4], wb1[64:128, z + 4:z + n + 4], ALU.add,
                         [wb1.k()], [wb2.k()])
                    wins = (8, 16)
                for half in range(2):
                    sl = slice(64 * half, 64 * half + 64)
                    F_ = finals[half]
                    P.stt(pl[sl, tk0:tk0 + n], F_[sl, z:z + n], 1.0 / wins[half], ub[sl, z:z + n], ALU.mult, ALU.subtract,
                          [F_.k(), ub.k()], [pl.k(c)])
                    for e_, (a0, b0) in enumerate(((0, 8), (n - 8, n))):
                        P.tt("dve", wb1[sl, 0:8] if F_ is wb2 else wb2[sl, 0:8], F_[sl, z + a0:z + b0], pedge[sl, c, e_, :], ALU.mult,
                             [F_.k(), pedge.k()], [(wb1 if F_ is wb2 else wb2).k()])
                        P.tt("dve", pl[sl, tk0 + a0:tk0 + b0], wb1[sl, 0:8] if F_ is wb2 else wb2[sl, 0:8], ub[sl, z + a0:z + b0],
                             ALU.subtract, [(wb1 if F_ is wb2 else wb2).k(), ub.k()], [pl.k(c)])
            for gi_, (t0, n) in enumerate(tgs):
                tl = tg_tiles(t0, n)
                bk = 2 + gi_ % 2
                P.mm(ps[:, bk, 0:n], PWt[:, c, :], pl[:, t0:t0 + n], True, True, [PWt.k(), pl.k(c)], [PSK(bk)])
                P.act(ycs[3][:, c, t0:t0 + n], ps[:, bk, 0:n], AF.Copy, [PSK(bk)] + PFK, [ycs[3].k(c, i) for i in tl],
                      scale=pscale[:, c:c + 1])
            A.release(mDc)
        A.release(mD)

        if STOP[0] == "D":
            A.release(mL)
            return
        P.phase = "L%d_wout" % l
        def ycat_keys(i):
            return [ycs[m_].k(c_, i) for m_ in range(4) for c_ in range(2)]

        make_gbc(16)

        mO = A.mark()
        WO = A.alloc("WO", [8, D], BF16)
        wov = wout_d[l].rearrange("(k p) n -> p k n", p=128)
        for q in range(2):
            wcast(WO[:, 4 * q:4 * q + 4, :], wov[:, 4 * q:4 * q + 4, :], [WO.k()])
        etmp = [A.alloc("etmp%d" % i, [512], F32) for i in range(2)]

        def epilogue(bank, i, nh, gidx):
            et = etmp[nh]
            which = 1 if i < 2 else 0
            P.tt("dve", et.ap, ps[:, bank, :], gbc[:, which, nh * 512:(nh + 1) * 512], ALU.mult,
                 [PSK(bank), gbc.k(which)], [et.k()])
            P.tt("pool", res[:, i, nh * 512:(nh + 1) * 512], res[:, i, nh * 512:(nh + 1) * 512], et.ap, ALU.add,
                 [res.k(i), et.k()], [res.k(i)])

        xn2 = [A.alloc("xn2_%d" % i, [2, D], BF16) for i in range(2)]
        n2steps = norm_steps(A2, 24, out_tiles, xn2, 4)
        for n_, i in enumerate(out_tiles):
            tok = slice(i * 128, (i + 1) * 128)
            for nh in range(2):
                bank = (n_ % 2) * 2 + nh
                for k in range(8):
                    P.mm(ps[:, bank, :], ycs[k // 2][:, k % 2, tok], WO[:, k, nh * 512:(nh + 1) * 512], k == 0, k == 7,
                         ycat_keys(i) + [WO.k()], [PSK(bank)])
                epilogue(bank, i, nh, 0)
            if n_ % 2 == 1:
                P.phase = "L%d_norm2" % l
                n2steps[n_ // 2]()
                P.phase = "L%d_wout" % l
        A.release(mO)
        A.release(mL)

        if STOP[0] == "wout":
            return
        make_gbc(40)
        P.phase = "L%d_ffn" % l
        mF = A.mark()
        wupv = wup_d[l].rearrange("(k p) n -> p k n", p=128)
        wdnv = wdn_d[l].rearrange("(j p) n -> p j n", p=128)
        NG = max(FFN_GROUPS)
        Wa = [A.alloc("Wa%d" % i, [8, NG * 128], BF16) for i in range(2)]
        Wg = [A.alloc("Wg%d" % i, [8, NG * 128], BF16) for i in range(2)]
        Wd = [A.alloc("Wd%d" % i, [NG, D], BF16) for i in range(2)]
        actT = [A.alloc("actT%d" % i, [NG, NT], BF16) for i in range(2)]
        ua = [A.alloc("ua%d" % i, [512], F32) for i in range(2)]
        ug = [A.alloc("ug%d" % i, [512], F32) for i in range(2)]
        etmp = [A.alloc("etmpF%d" % i, [512], F32) for i in range(2)]
        fgs = ([(0, TC, 0, TC)] if need_ctx else [])
        xs = [0, 510, 1020, 1530, 2040, 2048]
        for q in range(5):
            fgs.append((TC + xs[q], TC + xs[q + 1], TC, NT))

        def load_up(gidx):
            ng = FFN_GROUPS[gidx]
            j0 = sum(FFN_GROUPS[:gidx])
            b = gidx % 2
            wcast(Wa[b][:, :, 0:ng * 128], wupv[:, :, j0 * 128:(j0 + ng) * 128], [Wa[b].k()])
            wcast(Wg[b][:, :, 0:ng * 128], wupv[:, :, DFF + j0 * 128:DFF + (j0 + ng) * 128], [Wg[b].k()])

        def load_dn(gidx):
            ng = FFN_GROUPS[gidx]
            j0 = sum(FFN_GROUPS[:gidx])
            b = gidx % 2
            wcast(Wd[b][:, 0:ng, :], wdnv[:, j0:j0 + ng, :], [Wd[b].k()])

        pcount = [0]

        def phaseA_steps(gidx):
            ng = FFN_GROUPS[gidx]
            j0 = sum(FFN_GROUPS[:gidx])
            b = gidx % 2
            aT_ = actT[b]
            steps = []
            for (s, e_, lo, hi) in fgs:
                def step(s=s, e_=e_, lo=lo, hi=hi):
                    cs_, ce_ = max(s - 1, lo), min(e_ + 1, hi)
                    ncol = ce_ - cs_
                    nv = e_ - s
                    off = s - cs_
                    tl = list(range(cs_ // 128, (ce_ - 1) // 128 + 1))
                    wtl = list(range(s // 128, (e_ - 1) // 128 + 1))
                    for jl in range(ng):
                        pp = pcount[0] % 2
                        pcount[0] += 1
                        ba, bg = (0, 1) if pp == 0 else (2, 3)
                        ja, jg = j0 + jl, 22 + j0 + jl
                        for (bank, W_) in ((ba, Wa[b]), (bg, Wg[b])):
                            for k in range(8):
                                P.mm(ps[:, bank, 0:ncol], W_[:, k, jl * 128:(jl + 1) * 128], hT[:, k, cs_:ce_], k == 0, k == 7,
                                     [W_.k()] + [hT.k(i) for i in tl], [PSK(bank)])
                        for (bank, u_, jj) in ((ba, ua[pp], ja), (bg, ug[pp], jg)):
                            P.act(u_[:, 0:nv], ps[:, bank, off:off + nv], AF.Identity, [PSK(bank)] + PFK, [u_.k()],
                                  scale=fdw_w[:, jj, 1:2], bias=fdw_b[:, jj:jj + 1])
                            tl0 = max(s, lo + 1)
                            P.stt(u_[:, tl0 - s:nv], ps[:, bank, tl0 - 1 - cs_:e_ - 1 - cs_], fdw_w[:, jj, 0:1], u_[:, tl0 - s:nv],
                                  ALU.mult, ALU.add, [PSK(bank), u_.k()] + PFK, [u_.k()])
                            tr1 = min(e_, hi - 1)
                            P.stt(u_[:, 0:tr1 - s], ps[:, bank, s + 1 - cs_:tr1 + 1 - cs_], fdw_w[:, jj, 2:3], u_[:, 0:tr1 - s],
                                  ALU.mult, ALU.add, [PSK(bank), u_.k()] + PFK, [u_.k()])
                        P.act(ug[pp][:, 0:nv], ug[pp][:, 0:nv], AF.Silu, [ug[pp].k()], [ug[pp].k()])
                        P.tt("pool", aT_[:, jl, s:e_], ua[pp][:, 0:nv], ug[pp][:, 0:nv], ALU.mult, [ua[pp].k(), ug[pp].k()],
                             [aT_.k(jl, i) for i in wtl])
                steps.append(step)
            return steps

        def phaseB_steps(gidx):
            ng = FFN_GROUPS[gidx]
            b = gidx % 2
            aT_ = actT[b]
            steps = []
            for n_, i in enumerate(out_tiles):
                def step(n_=n_, i=i):
                    tok = slice(i * 128, (i + 1) * 128)
                    for nh in range(2):
                        bank = 4 + (n_ % 2) * 2 + nh
                        for jl in range(ng):
                            P.mm(ps[:, bank, :], aT_[:, jl, tok], Wd[b][:, jl, nh * 512:(nh + 1) * 512], jl == 0, jl == ng - 1,
                                 [aT_.k(jl, i), Wd[b].k()], [PSK(bank)])
                        epilogue(bank, i, nh, 1)
                steps.append(step)
            return steps

        nG = len(FFN_GROUPS)
        load_up(0)
        load_dn(0)
        prevB = []
        for gidx in range(nG):
            if gidx + 1 < nG:
                load_up(gidx + 1)
            a_steps = phaseA_steps(gidx)
            nb, na = len(prevB), len(a_steps)
            bi_ = 0
            for ai, st in enumerate(a_steps):
                st()
                tgt = (ai + 1) * nb // na
                while bi_ < tgt:
                    prevB[bi_]()
                    bi_ += 1
            if gidx + 1 < nG:
                load_dn(gidx + 1)
            prevB = phaseB_steps(gidx)
        for st in prevB:
            st()
        A.release(mF)

    for l in range(n_layers):
        layer(l)

    m = A.mark()
    fg = A.alloc("fg", [D], F32)
    P.dma(fg.ap, fg_d, writes=[fg.k()])
    ob = [A.alloc("ob%d" % i, [D], F32) for i in range(2)]
    ov = out_d.rearrange("(i p) d -> p i d", p=128)
    for n_, i in enumerate(range(2, NTILE)):
        P.act(ob[n_ % 2].ap, res[:, i, :], AF.Square, [res.k(i)], [ob[n_ % 2].k(), ss.k(i)], accum_out=ss[:, i:i + 1])
        P.act(rstd[:, i:i + 1], ss[:, i:i + 1], AF.Sqrt, [ss.k(i), epsc.k()], [rstd.k(i)], scale=1.0 / D, bias=epsc[:, 0:1])
        P.op("dve", lambda e, i=i: e.reciprocal(out=rstd[:, i:i + 1], in_=rstd[:, i:i + 1]), [rstd.k(i)], [rstd.k(i)])
        o = ob[n_ % 2]
        P.stt(o.ap, res[:, i, :], rstd[:, i:i + 1], fg.ap, ALU.mult, ALU.mult, [res.k(i), rstd.k(i), fg.k()], [o.k()])
        P.dma(ov[:, i - 2, :], o.ap, reads=[o.k()])
    if debug is not None:
        debug_fn[0](P, locals(), dbg_d)
    A.release(m)
    P.emit()
    es.close()
    print("program: %d instrs, arena peak %d KB" % (P.n, A.peak // 1024), flush=True)
    return nc


debug_fn = [None]
STOP = [None]
SCOPES = [False]
import os as _os
SKIP = set((_os.environ.get('BSKIP') or '').split(','))


def _fm(v):
    v = np.asarray(v, np.float32)
    return np.ascontiguousarray(v.reshape(-1, 128).T)


def host_consts():
    c = {}
    c["ident"] = np.eye(128, dtype=np.float32)
    t = np.arange(T)
    rows = (t // 64).astype(np.float64)
    cols = (t % 64).astype(np.float64)
    for name, q in (("ropeA", 8), ("ropeB", 16)):
        inv = 10000.0 ** (-np.arange(q, dtype=np.float64) / q)
        ang = np.stack([rows[:, None] * inv, cols[:, None] * inv], axis=1)
        tab = np.stack([np.cos(ang), np.sin(ang)], axis=1)
        tab = tab.reshape(16, 128, 2, 2 * q).transpose(1, 0, 2, 3)
        c[name] = np.ascontiguousarray(tab).astype(np.float32)
    j = np.arange(128)[:, None].astype(np.float32)
    i = np.arange(128)[None, :].astype(np.float32)
    relm = i - j
    c["rel"] = np.ascontiguousarray(np.stack([relm, -relm, (relm >= 0).astype(np.float32),
                                               (relm <= 0).astype(np.float32)], axis=1)).astype(np.float32)
    ii = np.arange(128, dtype=np.float32)
    c["iot"] = np.ascontiguousarray(np.broadcast_to(np.stack([ii + 1, 128 - ii], 0)[None], (128, 2, 128))).astype(np.float32)
    p = np.arange(128, dtype=np.float32)
    c["pcol"] = np.ascontiguousarray(np.stack([127 - p, p], axis=1)).astype(np.float32)
    pe = np.zeros((128, 2, 2, 8), np.float32)
    wins = (2, 4, 8, 16)
    n = 1 << 20
    for g, w in enumerate(wins):
        cc, half = g // 2, g % 2
        for e in range(8):
            tt = e
            cnt = min(tt + w // 2, n) - max(tt - w // 2, 0)
            pe[64 * half:64 * half + 64, cc, 0, e] = 1.0 / cnt
            tt = n - 8 + e
            cnt = min(tt + w // 2, n) - max(tt - w // 2, 0)
            pe[64 * half:64 * half + 64, cc, 1, e] = 1.0 / cnt
    c["pooledge"] = pe
    return c


def host_prepare(inputs):
    f = lambda k: np.asarray(inputs[k], np.float32)
    shared = {}
    for k in ("w_mod", "w_in", "w_out"):
        shared[k] = np.ascontiguousarray(f(k))
    shared["w_up"] = np.ascontiguousarray(f("ffn_w_up"))
    shared["w_down"] = np.ascontiguousarray(f("ffn_w_down"))
    shared["bmodT"] = np.ascontiguousarray(np.stack([_fm(f("b_mod")[l]) for l in range(DEPTH)], axis=1))
    pfm = np.zeros((128, DEPTH, NPF), np.float32)
    pbc = np.zeros((128, DEPTH, NPB), np.float32)
    poolw = np.zeros((DEPTH, 128, 2, 128), np.float32)
    for l in range(DEPTH):
        pfm[:, l, 0:8] = _fm(f("norm1_g")[l])
        pfm[:, l, 8:16] = _fm(f("norm2_g")[l])
        cw = f("conv_dw_w")[l]
        pfm[:, l, 16:78] = cw.T.reshape(2, 128, 31).transpose(1, 0, 2).reshape(128, 62)
        pfm[:, l, 78:80] = _fm(f("conv_dw_b")[l])
        pfm[:, l, 80:82] = _fm(f("conv_ln_g")[l])
        pfm[:, l, 82:84] = _fm(f("conv_ln_b")[l])
        pfm[:, l, 84:86] = _fm(f("pool_scale")[l])
        fw_ = f("ffn_dw_w")[l]
        pfm[:, l, 86:218] = fw_.T.reshape(44, 128, 3).transpose(1, 0, 2).reshape(128, 132)
        pfm[:, l, 218:262] = _fm(f("ffn_dw_b")[l])
        pbc[:, l, 0:64] = f("diff_subln_g")[l][None, :]
        pbc[:, l, 64:192] = f("diff_lambda")[l].reshape(1, 128)
        pbc[:, l, 192:200] = f("ret_decay")[l].reshape(1, 8)
        pw = f("pool_w")[l]
        for g in range(4):
            cc, half = g // 2, g % 2
            poolw[l, 64 * half:64 * half + 64, cc, 64 * half:64 * half + 64] = pw[g]
    shared["pfm"] = pfm
    shared["pbc"] = pbc
    shared["poolw"] = poolw
    shared["final_g_bc"] = np.ascontiguousarray(np.broadcast_to(f("final_g")[None, :], (128, D))).astype(np.float32)
    shared.update(host_consts())
    x, c, ctx, c_ctx = f("x"), f("c"), f("ctx"), f("c_ctx")
    in_maps = []
    for b in range(x.shape[0]):
        m = dict(shared)
        m["x"] = np.ascontiguousarray(x[b])
        m["ctx"] = np.ascontiguousarray(ctx[b])
        cc = np.stack([c[b], c_ctx], axis=0)
        m["ccT"] = np.ascontiguousarray(cc.reshape(2, 8, 128).transpose(2, 1, 0))
        in_maps.append(m)
    return in_maps


_NC_CACHE = {}


def kernel(**inputs):
    in_maps = host_prepare(inputs)
    if "nc" not in _NC_CACHE:
        _NC_CACHE["nc"] = build_program(DEPTH)
    nc = _NC_CACHE["nc"]
    res = run_bass_kernel_spmd(nc, in_maps, core_ids=list(range(len(in_maps))))
    out = np.stack([np.asarray(r["out"], np.float32) for r in res.results], axis=0)
    return out
```

```python
import math
import numpy as np
from contextlib import ExitStack
import concourse.bass as bass
import concourse.mybir as mybir
from concourse.bass_utils import run_bass_kernel_spmd

F32 = mybir.dt.float32
BF16 = mybir.dt.bfloat16
ALU = mybir.AluOpType
AF = mybir.ActivationFunctionType
AX = mybir.AxisListType

D = 1024
T = 2048
TC = 256
NT = T + TC
NTILE = NT // 128
DIN = 2560
DFF = 2816
DEPTH = 4
DGE_SCRATCH = 4096
EPS = 1e-6
NPF = 262
NPB = 200
FFN_GROUPS = [3, 3, 3, 3, 3, 3, 2, 2]

SAME_ENGINE_RAW = True


class Instr:
    __slots__ = ("eng", "fn", "deps", "idx", "is_dma", "sig", "count", "lane", "phase")

    def __init__(self, eng, fn, is_dma):
        self.eng = eng
        self.fn = fn
        self.deps = []
        self.is_dma = is_dma
        self.sig = False
        self.count = 0
        self.lane = None


class Prog:
    ENGS = ("pe", "dve", "act", "pool", "sp")

    def __init__(self, nc, n_lanes=20):
        self.nc = nc
        self.lists = {e: [] for e in self.ENGS}
        self.state = {}
        self.inherit = {}
        self.keys_by_name = {}
        self.n_lanes = n_lanes
        self.phase = None
        self.scopes = False
        self.lane_rr = {e: 0 for e in self.ENGS}
        self.lane_last = {}
        self.lane_cnt = {}
        self.n = 0

    def _st(self, k):
        st = self.state.get(k)
        if st is None:
            rd = {}
            if isinstance(k, tuple):
                inh = self.inherit.get(k[0])
                if inh:
                    rd = dict(inh)
                self.keys_by_name.setdefault(k[0], []).append(k)
            st = self.state[k] = [None, rd]
        return st

    def users_of(self, name):
        out = {}
        for k in self.keys_by_name.get(name, ()):
            st = self.state[k]
            cands = list(st[1].values())
            if st[0] is not None:
                cands.append(st[0])
            for ins in cands:
                rk = ("d", ins.idx) if ins.is_dma else ins.eng
                o = out.get(rk)
                if o is None or o.idx < ins.idx:
                    out[rk] = ins
        return out

    def _add(self, eng, fn, reads, writes, is_dma):
        ins = Instr(eng, fn, is_dma)
        ins.phase = self.phase
        ins.idx = self.n
        self.n += 1
        deps = []
        for k in reads:
            st = self._st(k)
            if st[0] is not None:
                deps.append((st[0], "raw"))
            if isinstance(k, tuple) and k[0] == "ps":
                for r in st[1].values():
                    if r.eng != eng:
                        deps.append((r, "x"))
        for k in writes:
            st = self._st(k)
            if st[0] is not None:
                deps.append((st[0], "waw"))
            for r in st[1].values():
                deps.append((r, "war"))
        rk = ("d", ins.idx) if is_dma else eng
        for k in reads:
            self.state[k][1][rk] = ins
        for k in writes:
            st = self.state[k]
            st[0] = ins
            st[1] = {}
        seen = set()
        for p, kind in deps:
            if p is ins or id(p) in seen:
                continue
            if (not p.is_dma) and p.eng == eng and not is_dma:
                if eng == "pe":
                    continue
                if not SAME_ENGINE_RAW:
                    continue
            seen.add(id(p))
            ins.deps.append(p)
            p.sig = True
        if is_dma:
            key = (eng, self.lane_rr[eng] % self.n_lanes)
            self.lane_rr[eng] += 1
            ins.lane = key
            prev = self.lane_last.get(key)
            if prev is not None:
                ins.deps.append(prev)
            self.lane_last[key] = ins
            self.lane_cnt[key] = self.lane_cnt.get(key, 0) + 16
            ins.count = self.lane_cnt[key]
            ins.sig = True
        self.lists[eng].append(ins)
        return ins

    def op(self, eng, fn, reads=(), writes=()):
        return self._add(eng, fn, tuple(reads), tuple(writes), False)

    def dma(self, out, in_, reads=(), writes=(), eng="sp", **kw):
        fn = lambda e: e.dma_start(out=out, in_=in_, **kw)
        return self._add(eng, fn, tuple(reads), tuple(writes), True)

    def mm(self, out, lhsT, rhs, start, stop, reads, writes, **kw):
        return self.op("pe", lambda e: e.matmul(out, lhsT, rhs, start=start, stop=stop, **kw), reads, writes)

    def tr(self, out, in_, ident, reads, writes):
        return self.op("pe", lambda e: e.transpose(out, in_, ident), reads, writes)

    def act(self, out, in_, func, reads, writes, **kw):
        return self.op("act", lambda e: e.activation(out=out, in_=in_, func=func, **kw), reads, writes)

    def tt(self, eng, out, in0, in1, op, reads, writes):
        return self.op(eng, lambda e: e.tensor_tensor(out=out, in0=in0, in1=in1, op=op), reads, writes)

    def ts(self, eng, out, in0, s1, s2, op0, op1, reads, writes):
        if op1 is None:
            return self.op(eng, lambda e: e.tensor_scalar(out=out, in0=in0, scalar1=s1, scalar2=None, op0=op0), reads, writes)
        return self.op(eng, lambda e: e.tensor_scalar(out=out, in0=in0, scalar1=s1, scalar2=s2, op0=op0, op1=op1), reads, writes)

    def stt(self, out, in0, scalar, in1, op0, op1, reads, writes):
        return self.op("dve", lambda e: e.scalar_tensor_tensor(out=out, in0=in0, scalar=scalar, in1=in1, op0=op0, op1=op1), reads, writes)

    def cp(self, eng, out, in_, reads, writes):
        if eng == "act":
            return self.op("act", lambda e: e.copy(out=out, in_=in_), reads, writes)
        return self.op(eng, lambda e: e.tensor_copy(out=out, in_=in_), reads, writes)

    def emit(self):
        nc = self.nc
        with ExitStack() as es:
            esem = {}
            for e in ("pe", "dve", "act", "pool"):
                esem[e] = es.enter_context(nc.semaphore("s_" + e))
            lsem = {}
            for key in self.lane_cnt:
                lsem[key] = es.enter_context(nc.semaphore("l_%s_%d" % key))
            for e in ("pe", "dve", "act", "pool"):
                c = 0
                for ins in self.lists[e]:
                    if (not ins.is_dma) and ins.sig:
                        c += 1
                        ins.count = c
            block = es.enter_context(nc.Block())

            def replay(ename, eh):
                known = {}
                cur = None
                for ins in self.lists[ename]:
                    if self.scopes and ins.phase != cur:
                        if cur is not None:
                            nc.leave_named_scope(cur, sid, False)
                        cur = ins.phase
                        if cur is not None:
                            sid, _ = nc.enter_named_scope(cur, False)
                    for p in ins.deps:
                        if p.is_dma:
                            s, sk = lsem[p.lane], p.lane
                        else:
                            s, sk = esem[p.eng], p.eng
                        if known.get(sk, 0) >= p.count:
                            continue
                        known[sk] = p.count
                        eh.wait_ge(s, p.count)
                    r = ins.fn(eh)
                    if ins.is_dma:
                        r.then_inc(lsem[ins.lane], 16)
                    elif ins.sig:
                        r.then_inc(esem[ename], 1)
                if self.scopes and cur is not None:
                    nc.leave_named_scope(cur, sid, False)
                for key, last in self.lane_last.items():
                    if key[0] == ename and known.get(key, 0) < last.count:
                        eh.wait_ge(lsem[key], last.count)

            @block.tensor
            def _(e):
                replay("pe", e)

            @block.vector
            def _(e):
                replay("dve", e)

            @block.scalar
            def _(e):
                replay("act", e)

            @block.gpsimd
            def _(e):
                replay("pool", e)

            @block.sync
            def _(e):
                replay("sp", e)


class Buf:
    def __init__(self, name, ap):
        self.name = name
        self.ap = ap

    def __getitem__(self, idx):
        return self.ap[idx]

    def k(self, *i):
        return (self.name,) + tuple(i)


class Arena:
    def __init__(self, P, base_ap, nbytes):
        self.P = P
        self.base = base_ap
        self.nbytes = nbytes
        self.top = 0
        self.live = []
        self.dead = []
        self.gen = 0
        self.peak = 0

    def alloc(self, name, shape, dtype):
        esz = 4 if dtype == F32 else 2
        n = 1
        for s in shape:
            n *= s
        nb = (n * esz + 63) // 64 * 64
        start = self.top
        end = start + nb
        assert end <= self.nbytes, "arena overflow %s need %d have %d" % (name, end, self.nbytes)
        self.top = end
        self.peak = max(self.peak, end)
        self.gen += 1
        uname = "%s#%d" % (name, self.gen)
        inh = {}
        for (s0, e0, nm) in self.dead:
            if s0 < end and start < e0:
                for rk, ins in self.P.users_of(nm).items():
                    o = inh.get(rk)
                    if o is None or o.idx < ins.idx:
                        inh[rk] = ins
        if inh:
            self.P.inherit[uname] = inh
        ap = self.base[:, start // 2: start // 2 + n * esz // 2]
        if dtype == F32:
            ap = ap.bitcast(F32)
        if len(shape) > 1:
            names = " ".join("d%d" % i for i in range(len(shape)))
            kw = {"d%d" % i: shape[i] for i in range(len(shape))}
            ap = ap.rearrange("p (%s) -> p %s" % (names, names), **kw)
        self.live.append((start, end, uname))
        return Buf(uname, ap)

    def mark(self):
        return (self.top, len(self.live))

    def release(self, m):
        top, nl = m
        while len(self.live) > nl:
            self.dead.append(self.live.pop())
        self.top = top


def build_program(n_layers=DEPTH, debug=None):
    nc = bass.Bass("TRN2", target_bir_lowering=False, dynamic_dma_scratch_size=DGE_SCRATCH)
    dt = lambda name, shape, kind="ExternalInput": nc.dram_tensor(name, shape, F32, kind=kind).ap()
    x_d = dt("x", [T, D])
    ctx_d = dt("ctx", [TC, D])
    cc_d = dt("ccT", [128, 8, 2])
    wmod_d = dt("w_mod", [DEPTH, D, 6 * D])
    bmod_d = dt("bmodT", [128, DEPTH, 48])
    win_d = dt("w_in", [DEPTH, D, DIN])
    wout_d = dt("w_out", [DEPTH, D, D])
    wup_d = dt("w_up", [DEPTH, D, 2 * DFF])
    wdn_d = dt("w_down", [DEPTH, DFF, D])
    pfm_d = dt("pfm", [128, DEPTH, NPF])
    pbc_d = dt("pbc", [128, DEPTH, NPB])
    poolw_d = dt("poolw", [DEPTH, 128, 2, 128])
    fg_d = dt("final_g_bc", [128, D])
    ident_d = dt("ident", [128, 128])
    ropeA_d = dt("ropeA", [128, 16, 2, 16])
    ropeB_d = dt("ropeB", [128, 16, 2, 32])
    rel_d = dt("rel", [128, 4, 128])
    iot_d = dt("iot", [128, 2, 128])
    pcol_d = dt("pcol", [128, 2])
    pedge_d = dt("pooledge", [128, 2, 2, 8])
    out_d = dt("out", [T, D], kind="ExternalOutput")
    dbg_d = None
    if debug is not None:
        dbg_d = dt("dbg", list(debug), kind="ExternalOutput")

    es = ExitStack()
    ARENA_BYTES = (219 - (DGE_SCRATCH - 4096) // 1024) * 1024
    big = es.enter_context(nc.sbuf_tensor("arena", [128, ARENA_BYTES // 2], BF16))
    ps = es.enter_context(nc.psum_tensor("ps", [128, 8, 512], F32))
    P = Prog(nc)
    P.scopes = SCOPES[0]
    A = Arena(P, big[:], ARENA_BYTES)

    def PSK(b):
        return ("ps", b)

    def psb(b):
        return ps[:, b, :].bitcast(BF16)

    res = A.alloc("res", [NTILE, D], F32)
    hT = A.alloc("hT", [8, NT], BF16)
    ident_f = A.alloc("ident_f", [128], F32)
    ident_b = A.alloc("ident_b", [128], BF16)
    ones_f = A.alloc("ones_f", [128], F32)
    ropeA = A.alloc("ropeA", [16, 2, 16], F32)
    ropeB = A.alloc("ropeB", [16, 2, 32], F32)
    rel = A.alloc("rel", [4, 128], F32)
    iot = A.alloc("iot", [2, 128], F32)
    pcol = A.alloc("pcol", [2], F32)
    pedge = A.alloc("pedge", [2, 2, 8], F32)
    pfm = A.alloc("pfm", [DEPTH, NPF], F32)
    pbc = A.alloc("pbc", [DEPTH, NPB], F32)
    bmodT = A.alloc("bmodT", [DEPTH, 48], F32)
    scT = A.alloc("scT", [8, 2], F32)
    epsc = A.alloc("epsc", [1], F32)
    modT = A.alloc("modT", [48, 2], F32)
    modN = A.alloc("modN", [48, 2], F32)
    A1 = A.alloc("A1", [8, 2], F32)
    A2 = A.alloc("A2", [8, 2], F32)
    gbc = A.alloc("gbc", [2, D], F32)
    ss = A.alloc("ss", [NTILE], F32)
    rstd = A.alloc("rstd", [NTILE], F32)
    lg = A.alloc("lg", [8], F32)
    lgsel = A.alloc("lgsel", [2, 2], F32)
    g128 = A.alloc("g128", [2, 2], F32)
    kdec = A.alloc("kdec", [2, 4], F32)
    DTt = A.alloc("DTt", [4, 128], F32)
    TFB = A.alloc("TFB", [2, 2, 128], F32)
    lamt = A.alloc("lamt", [4], F32)
    sublnS = A.alloc("sublnS", [64], F32)

    xv = x_d.rearrange("(i p) d -> p i d", p=128)
    for q in range(4):
        P.dma(res[:, 2 + 4 * q: 6 + 4 * q, :], xv[:, 4 * q: 4 * q + 4, :],
              writes=[res.k(2 + 4 * q + j) for j in range(4)])
    P.dma(res[:, 0:2, :], ctx_d.rearrange("(i p) d -> p i d", p=128), writes=[res.k(0), res.k(1)])
    for buf, src in ((ident_f, ident_d), (ropeA, ropeA_d), (ropeB, ropeB_d), (rel, rel_d), (iot, iot_d),
                     (pcol, pcol_d), (pedge, pedge_d), (pfm, pfm_d), (pbc, pbc_d), (bmodT, bmod_d), (scT, cc_d)):
        P.dma(buf.ap, src, writes=[buf.k()])
    P.op("dve", lambda e: e.memset(ones_f.ap, 1.0), writes=[ones_f.k()])
    P.op("dve", lambda e: e.memset(epsc.ap, EPS), writes=[epsc.k()])
    P.cp("dve", ident_b.ap, ident_f.ap, [ident_f.k()], [ident_b.k()])
    P.act(scT.ap, scT.ap, AF.Silu, [scT.k()], [scT.k()])

    def wcast(dst_ap, src_ap, writes):
        return P.dma(dst_ap, src_ap, writes=writes, eng="pool")

    def layer(l):
        need_ctx = l < DEPTH - 1
        lambda_init = 0.8 - 0.6 * math.exp(-0.3 * l)
        pf = lambda c0, c1: pfm[:, l, c0:c1]
        n1g, n2g = pf(0, 8), pf(8, 16)
        cdw_w = pf(16, 78).rearrange("p (c k) -> p c k", c=2)
        cdw_b, cln_g, cln_b, pscale = pf(78, 80), pf(80, 82), pf(82, 84), pf(84, 86)
        fdw_w = pf(86, 218).rearrange("p (j k) -> p j k", k=3)
        fdw_b = pf(218, 262)
        PFK = [pfm.k()]
        PBK = [pbc.k()]
        tiles_all = list(range(NTILE))
        tiles_x = list(range(2, NTILE))
        out_tiles = tiles_all if need_ctx else tiles_x

        P.phase = "L%d_mod" % l
        m0 = A.mark()
        if l == 0:
            wmv = wmod_d[l].rearrange("(k p) n -> p k n", p=128)
            wm = [A.alloc("wm%d" % i, [8, 512], F32) for i in range(2)]
            for pc in range(12):
                w = wm[pc % 2]
                P.dma(w.ap, wmv[:, :, pc * 512:(pc + 1) * 512], writes=[w.k()])
                for jj in range(4):
                    col = (pc * 4 + jj) * 2
                    for k in range(8):
                        P.mm(ps[:, 0, col:col + 2], w[:, k, jj * 128:(jj + 1) * 128], scT[:, k, :],
                             k == 0, k == 7, [w.k(), scT.k()], [PSK(0)])
            P.tt("dve", modT.ap, ps[:, 0, 0:96].rearrange("p (c w) -> p c w", w=2),
                 bmodT[:, l, :].unsqueeze(2).to_broadcast([128, 48, 2]), ALU.add,
                 [PSK(0), bmodT.k()], [modT.k()])
        else:
            P.cp("dve", modT.ap, modN.ap, [modN.k()], [modT.k()])
        for (Ax, ng, c0) in ((A1, n1g, 8), (A2, n2g, 32)):
            P.ts("dve", Ax.ap, modT[:, c0:c0 + 8, :], 1.0, None, ALU.add, None, [modT.k()], [Ax.k()])
            P.tt("dve", Ax.ap, Ax.ap, ng.unsqueeze(2).to_broadcast([128, 8, 2]), ALU.mult,
                 [Ax.k()] + PFK, [Ax.k()])
        A.release(m0)

        def make_gbc(c0):
            mg = A.mark()
            dg = A.alloc("dg", [8, 128], F32)
            for which in range(2):
                if which == 1 and not need_ctx:
                    continue
                for j in range(8):
                    P.ts("dve", dg[:, j, :], ident_f.ap, modT[:, c0 + j, which:which + 1], None, ALU.mult, None,
                         [ident_f.k(), modT.k()], [dg.k()])
                for half in range(2):
                    b = 1 + half
                    P.mm(ps[:, b, :], ones_f.ap, dg[:, 4 * half:4 * half + 4, :].rearrange("p a b -> p (a b)"),
                         True, True, [ones_f.k(), dg.k()], [PSK(b)])
                    P.cp("act", gbc[:, which, half * 512:(half + 1) * 512], ps[:, b, :],
                         [PSK(b)], [gbc.k(which)])
            A.release(mg)

        m0 = A.mark()
        A.release(m0)

        if STOP[0] == "mod":
            return
        P.phase = "L%d_tables" % l
        m0 = A.mark()
        tmpl = A.alloc("tmpl", [2, 32], F32)
        dl = pbc[:, l, 64:192].rearrange("p (a b c) -> p a b c", a=2, b=2)
        P.tt("dve", tmpl.ap, dl[:, :, 0, :], dl[:, :, 1, :], ALU.mult, PBK, [tmpl.k()])
        P.op("dve", lambda e: e.tensor_reduce(out=lamt[:, 0:2], in_=tmpl.ap, axis=AX.X, op=ALU.add), [tmpl.k()], [lamt.k()])
        P.act(lamt[:, 0:2], lamt[:, 0:2], AF.Exp, [lamt.k()], [lamt.k()])
        P.tt("dve", lamt[:, 2:3], lamt[:, 1:2], lamt[:, 0:1], ALU.subtract, [lamt.k()], [lamt.k()])
        P.ts("dve", lamt[:, 3:4], lamt[:, 2:3], -lambda_init, None, ALU.add, None, [lamt.k()], [lamt.k()])
        P.ts("dve", sublnS.ap, pbc[:, l, 0:64], 1.0 - lambda_init, None, ALU.mult, None, PBK, [sublnS.k()])
        P.act(lg.ap, pbc[:, l, 192:200], AF.Exp, PBK, [lg.k()])
        P.ts("dve", lg.ap, lg.ap, -1.0, None, ALU.mult, None, [lg.k()], [lg.k()])
        for hp in range(2):
            for d_ in range(2):
                for half in range(2):
                    sl = slice(64 * half, 64 * half + 64)
                    P.cp("dve", lgsel[sl, hp, d_:d_ + 1], lg[sl, d_ * 4 + 2 * hp + half: d_ * 4 + 2 * hp + half + 1],
                         [lg.k()], [lgsel.k()])
        P.act(g128.ap, lgsel.ap, AF.Exp, [lgsel.k()], [g128.k()], scale=128.0)
        for d_ in range(2):
            P.act(kdec[:, d_, :], lg[:, d_ * 4:d_ * 4 + 4], AF.Exp, [lg.k(), pcol.k()], [kdec.k()], scale=pcol[:, d_:d_ + 1])
        P.ts("dve", kdec.ap, kdec.ap, 0.125, None, ALU.mult, None, [kdec.k()], [kdec.k()])
        tmpd = A.alloc("tmpd", [2, 128], F32)
        for h in range(4):
            P.act(tmpd[:, 0, :], rel[:, 0, :], AF.Exp, [rel.k(), lg.k()], [tmpd.k()], scale=lg[:, h:h + 1])
            P.act(tmpd[:, 1, :], rel[:, 1, :], AF.Exp, [rel.k(), lg.k()], [tmpd.k()], scale=lg[:, 4 + h:5 + h])
            P.tt("dve", tmpd.ap, tmpd.ap, rel[:, 2:4, :], ALU.mult, [tmpd.k(), rel.k()], [tmpd.k()])
            P.tt("dve", DTt[:, h, :], tmpd[:, 0, :], tmpd[:, 1, :], ALU.add, [tmpd.k()], [DTt.k()])
        P.ts("dve", DTt.ap, DTt.ap, 0.125, None, ALU.mult, None, [DTt.k()], [DTt.k()])
        for hp in range(2):
            for d_ in range(2):
                P.act(TFB[:, hp, d_, :], iot[:, d_, :], AF.Exp, [iot.k(), lgsel.k()], [TFB.k()],
                      scale=lgsel[:, hp, d_:d_ + 1])
        A.release(m0)

        if STOP[0] == "tables":
            return
        def norm_steps(Ax, bcol, tiles, xn, bank0):
            steps = []
            for gi_, g0 in enumerate(range(0, len(tiles), 2)):
                def step(gi_=gi_, g0=g0):
                    grp = tiles[g0:g0 + 2]
                    xb = xn[gi_ % 2]
                    b0 = bank0 + 2 * (gi_ % 2)
                    for s, i in enumerate(grp):
                        P.act(xb[:, s, :], res[:, i, :], AF.Square, [res.k(i)], [xb.k(s), ss.k(i)], accum_out=ss[:, i:i + 1])
                        P.act(rstd[:, i:i + 1], ss[:, i:i + 1], AF.Sqrt, [ss.k(i), epsc.k()], [rstd.k(i)],
                              scale=1.0 / D, bias=epsc[:, 0:1])
                        P.op("dve", lambda e, i=i: e.reciprocal(out=rstd[:, i:i + 1], in_=rstd[:, i:i + 1]),
                             [rstd.k(i)], [rstd.k(i)])
                        P.ts("dve", xb[:, s, :], res[:, i, :], rstd[:, i:i + 1], None, ALU.mult, None,
                             [res.k(i), rstd.k(i)], [xb.k(s)])
                        for j in range(8):
                            bank = b0 + j // 4
                            o = psb(bank)[:, (j % 4) * 256 + s * 128:(j % 4) * 256 + s * 128 + 128]
                            P.tr(o, xb[:, s, j * 128:(j + 1) * 128], ident_b.ap, [xb.k(s), ident_b.k()], [PSK(bank)])
                    which = 1 if grp[0] < 2 else 0
                    t0 = grp[0] * 128
                    for j in range(8):
                        bank = b0 + j // 4
                        src = psb(bank)[:, (j % 4) * 256:(j % 4) * 256 + 256]
                        dst = hT[:, j, t0:t0 + 256]
                        rk = [PSK(bank), Ax.k(), modT.k()]
                        wk = [hT.k(grp[0]), hT.k(grp[1])]
                        if j % 2 == 0:
                            P.ts("dve", dst, src, Ax[:, j, which:which + 1], modT[:, bcol + j, which:which + 1],
                                 ALU.mult, ALU.add, rk, wk)
                        else:
                            P.act(dst, src, AF.Identity, rk, wk, scale=Ax[:, j, which:which + 1],
                                  bias=modT[:, bcol + j, which:which + 1])
                steps.append(step)
            return steps

        def norm_mod(Ax, bcol, tiles):
            m = A.mark()
            xn = [A.alloc("xn%d" % i, [2, D], BF16) for i in range(2)]
            for st in norm_steps(Ax, bcol, tiles, xn, 0):
                st()
            A.release(m)

        norm_mod(A1, 0, tiles_all)

        if STOP[0] == "norm1":
            return
        win_v = win_d[l].rearrange("(k p) n -> p k n", p=128)
        mL = A.mark()
        ycs = [None] * 4
        ycs[0] = A.alloc("ycA", [2, NT], BF16)

        P.phase = "L%d_Aproj" % l
        mA = A.mark()
        qkT = A.alloc("qkT", [4, NT], BF16)
        vaug = A.alloc("vaug", [NTILE, 4, 65], BF16)
        P.op("pool", lambda e: e.memset(vaug.ap, 1.0), [], [vaug.k(i) for i in range(NTILE)])
        mA2 = A.mark()
        WA = A.alloc("WA", [8, 768], BF16)
        wcast(WA[:, :, 0:256], win_v[:, :, 0:256], [WA.k()])
        wcast(WA[:, :, 256:768], win_v[:, :, 1536:2048], [WA.k()])
        stg = [A.alloc("stgA%d" % i, [512], BF16) for i in range(2)]
        rt = [A.alloc("rtA%d" % i, [256], F32) for i in range(4)]
        ptiles = tiles_all
        for n_, i in enumerate(ptiles):
            bA, bB = (0, 1) if n_ % 2 == 0 else (2, 3)
            tok = slice(i * 128, (i + 1) * 128)
            need_q = (i >= 2) or need_ctx
            c0 = 0 if need_q else 256
            for k in range(8):
                P.mm(ps[:, bA, c0:512], hT[:, k, tok], WA[:, k, c0:512], k == 0, k == 7, [hT.k(i), WA.k()], [PSK(bA)])
            for k in range(8):
                P.mm(ps[:, bB, 0:256], hT[:, k, tok], WA[:, k, 512:768], k == 0, k == 7, [hT.k(i), WA.k()], [PSK(bB)])
            sg = stg[n_ % 2]
            if i >= 2:
                xi = i - 2
                v = ps[:, bA, :].rearrange("p (g i j r) -> p g i j r", g=16, i=2, j=2)
                a_, b_ = v[:, :, :, 0, :], v[:, :, :, 1, :]
                sv = sg.ap.rearrange("p (g i j r) -> p g i j r", g=16, i=2, j=2)
                cs = ropeA[:, xi, 0, :].rearrange("p (i r) -> p i r", i=2).unsqueeze(1).to_broadcast([128, 16, 2, 8])
                sn = ropeA[:, xi, 1, :].rearrange("p (i r) -> p i r", i=2).unsqueeze(1).to_broadcast([128, 16, 2, 8])
                r4 = [r_.ap.rearrange("p (g i r) -> p g i r", g=16, i=2) for r_ in rt]
                RK = [PSK(bA), ropeA.k()]
                P.tt("dve", r4[0], a_, cs, ALU.mult, RK, [rt[0].k()])
                P.tt("dve", r4[1], b_, sn, ALU.mult, RK, [rt[1].k()])
                P.tt("dve", r4[2], a_, sn, ALU.mult, RK, [rt[2].k()])
                P.tt("dve", r4[3], b_, cs, ALU.mult, RK, [rt[3].k()])
                P.tt("pool", sv[:, :, :, 0, :], r4[0], r4[1], ALU.subtract, [rt[0].k(), rt[1].k()], [sg.k()])
                P.tt("pool", sv[:, :, :, 1, :], r4[2], r4[3], ALU.add, [rt[2].k(), rt[3].k()], [sg.k()])
            else:
                P.cp("act", sg[:, c0:512], ps[:, bA, c0:512], [PSK(bA)], [sg.k()])
            P.cp("act", vaug[:, i, :, 0:64], ps[:, bB, 0:256].rearrange("p (h d) -> p h d", h=4), [PSK(bB)], [vaug.k(i)])
            bT = 4 + n_ % 2
            cl = list(range(4)) if need_q else [2, 3]
            for c in cl:
                P.tr(psb(bT)[:, c * 128:(c + 1) * 128], sg[:, c * 128:(c + 1) * 128], ident_b.ap,
                     [sg.k(), ident_b.k()], [PSK(bT)])
            P.cp("act", qkT[:, cl[0]:4, tok], psb(bT)[:, cl[0] * 128:512].rearrange("p (c t) -> p c t", t=128),
                 [PSK(bT)], [qkT.k(i)])
        A.release(mA2)

        if STOP[0] == "A_proj":
            A.release(mL)
            return
        P.phase = "L%d_Amain" % l
        PT = [A.alloc("PT%d" % i, [512], BF16) for i in range(3)]
        oA = A.alloc("oA", [4, 256], F32)
        rcp = A.alloc("rcp", [2, 4], F32)
        otmp = A.alloc("otmp", [64], F32)
        sqA = A.alloc("sqA", [256], F32)
        ssA = A.alloc("ssA", [4], F32)
        yA = A.alloc("yA", [256], BF16)
        sc_exp = 32.0 ** -0.5
        acc_ctr = [0]
        do_pref = (l + 1 < n_layers)
        pref = {"pc": 0, "dma": 0, "cnt": 0}
        if do_pref:
            wmv_n = wmod_d[l + 1].rearrange("(k p) n -> p k n", p=128)
            wmp = [A.alloc("wmp%d" % i, [8, 256], F32) for i in range(2)]

            def pref_dma():
                pc = pref["dma"]
                if pc < 24:
                    w = wmp[pc % 2]
                    P.dma(w.ap, wmv_n[:, :, pc * 256:(pc + 1) * 256], writes=[w.k()])
                    pref["dma"] += 1

            def pref_piece():
                pc = pref["pc"]
                if pc >= 24:
                    return
                w = wmp[pc % 2]
                for jj in range(2):
                    col = 416 + (pc * 2 + jj) * 2
                    for k in range(8):
                        P.mm(ps[:, 7, col:col + 2], w[:, k, jj * 128:(jj + 1) * 128], scT[:, k, :], k == 0, k == 7,
                             [w.k(), scT.k()], [PSK(7)])
                pref["pc"] += 1
                pref_dma()

            pref_dma()
            pref_dma()

        def pref_tick():
            if not do_pref:
                return
            pref["cnt"] += 1
            if pref["cnt"] % 22 == 0:
                pref_piece()

        def attn_block(q0, nq, ktiles):
            nqb = nq // 128
            qtile0 = q0 // 128
            for h in range(4):
                ab = (3, 4) if acc_ctr[0] % 2 == 0 else (5, 6)
                acc_ctr[0] += 1
                steps = [(kt, m) for kt in ktiles for m in range(2)]
                started = set()

                def qk(sidx):
                    kt, m = steps[sidx]
                    rb = (h % 2) * 64 + m * 32
                    bS = sidx % 3
                    P.mm(ps[:, bS, 0:nq], qkT[rb:rb + 32, 2 + h // 2, kt * 128:(kt + 1) * 128],
                         qkT[rb:rb + 32, h // 2, q0:q0 + nq], True, True,
                         [qkT.k(kt)] + [qkT.k(qtile0 + j) for j in range(nqb)], [PSK(bS)], tile_position=(rb, 0))
                    P.act(PT[bS][:, 0:nq], ps[:, bS, 0:nq], AF.Exp, [PSK(bS)], [PT[bS].k()], scale=sc_exp)

                def pv(sidx):
                    kt, m = steps[sidx]
                    bS = sidx % 3
                    for qb in range(nqb):
                        bank = ab[qb // 2]
                        col = ((qb % 2) * 2 + m) * 65
                        st = bank not in started
                        started.add(bank)
                        P.mm(ps[:, bank, col:col + 65], PT[bS][:, qb * 128:(qb + 1) * 128], vaug[:, kt, h, :],
                             st, kt == ktiles[-1], [PT[bS].k(), vaug.k(kt)], [PSK(bank)], skip_group_check=True)

                for s_ in range(len(steps) + 2):
                    if s_ < len(steps):
                        qk(s_)
                    if s_ >= 2:
                        pv(s_ - 2)
                    pref_tick()
                for bi in range((nqb + 1) // 2):
                    bank = ab[bi]
                    av = ps[:, bank, 0:260].rearrange("p (g c) -> p g c", c=65)
                    P.op("dve", lambda e, av=av, bi=bi: e.reciprocal(out=rcp[:, bi, :], in_=av[:, :, 64]),
                         [PSK(bank)], [rcp.k()])
                    rv = rcp[:, bi, :].rearrange("p (q m) -> p q m", m=2)
                    P.ts("dve", rv[:, :, 1], rv[:, :, 1], lamt[:, 3:4], None, ALU.mult, None, [rcp.k(), lamt.k()], [rcp.k()])
                    for qq in range(2):
                        qb = bi * 2 + qq
                        if qb >= nqb:
                            continue
                        P.ts("dve", otmp.ap, av[:, qq * 2, 0:64], rcp[:, bi, qq * 2:qq * 2 + 1], None, ALU.mult, None,
                             [PSK(bank), rcp.k()], [otmp.k()])
                        P.stt(oA[:, qb, h * 64:(h + 1) * 64], av[:, qq * 2 + 1, 0:64], rcp[:, bi, qq * 2 + 1:qq * 2 + 2],
                              otmp.ap, ALU.mult, ALU.add, [PSK(bank), rcp.k(), otmp.k()], [oA.k(qb)])
            for qb in range(nqb):
                ti = qtile0 + qb
                o4 = oA[:, qb, :].rearrange("p (h d) -> p h d", h=4)
                P.tt("pool", sqA.ap, oA[:, qb, :], oA[:, qb, :], ALU.mult, [oA.k(qb)], [sqA.k()])
                P.op("dve", lambda e: e.tensor_reduce(out=ssA.ap, in_=sqA.ap.rearrange("p (h d) -> p h d", h=4), axis=AX.X, op=ALU.add),
                     [sqA.k()], [ssA.k()])
                P.act(ssA.ap, ssA.ap, AF.Sqrt, [ssA.k(), epsc.k()], [ssA.k()], scale=1.0 / 64, bias=epsc[:, 0:1])
                P.op("dve", lambda e: e.reciprocal(out=ssA.ap, in_=ssA.ap), [ssA.k()], [ssA.k()])
                s4 = sqA.ap.rearrange("p (h d) -> p h d", h=4)
                P.tt("dve", s4, o4, ssA.ap.unsqueeze(2).to_broadcast([128, 4, 64]), ALU.mult, [oA.k(qb), ssA.k()], [sqA.k()])
                P.tt("dve", yA.ap.rearrange("p (h d) -> p h d", h=4), s4, sublnS.ap.unsqueeze(1).to_broadcast([128, 4, 64]),
                     ALU.mult, [sqA.k(), sublnS.k()], [yA.k()])
                for c in range(2):
                    P.tr(psb(7)[:, c * 128:(c + 1) * 128], yA[:, c * 128:(c + 1) * 128], ident_b.ap, [yA.k(), ident_b.k()], [PSK(7)])
                P.cp("act", ycs[0][:, 0:2, ti * 128:(ti + 1) * 128], psb(7)[:, 0:256].rearrange("p (c t) -> p c t", t=128),
                     [PSK(7)], [ycs[0].k(0, ti), ycs[0].k(1, ti)])

        if need_ctx:
            attn_block(0, 256, [0, 1])
        for qt in range(4):
            attn_block(256 + qt * 512, 512, tiles_all)
        if do_pref:
            while pref["pc"] < 24:
                pref_piece()
            P.tt("dve", modN.ap, ps[:, 7, 416:512].rearrange("p (c w) -> p c w", w=2),
                 bmodT[:, l + 1, :].unsqueeze(2).to_broadcast([128, 48, 2]), ALU.add,
                 [PSK(7), bmodT.k()], [modN.k()])
        A.release(mA)

        if STOP[0] == "A":
            A.release(mL)
            return
        P.phase = "L%d_B" % l
        ycs[1] = A.alloc("ycB", [2, NT], BF16)
        for hp in range(2):
            mB = A.mark()
            qT = A.alloc("qT", [NT], BF16)
            kT = A.alloc("kT", [NT], BF16)
            vt = A.alloc("vt", [NTILE, 128], BF16)
            Sall = A.alloc("Sall", [NTILE, 2, 64], F32)
            SallB = A.alloc("SallB", [NTILE, 2, 64], BF16)
            WB = A.alloc("WB", [8, 512], BF16)
            mB2 = A.mark()
            for ci, cb in enumerate((256, 2048, 2304, 512)):
                wcast(WB[:, :, ci * 128:(ci + 1) * 128], win_v[:, :, cb + hp * 128: cb + hp * 128 + 128], [WB.k()])
            fo = list(range(NTILE))
            bo = [1, 0] + list(range(NTILE - 1, 1, -1))
            nxt_of = [{fo[a]: fo[a + 1] for a in range(NTILE - 1)}, {bo[a]: bo[a + 1] for a in range(NTILE - 1)}]
            stg = [A.alloc("stgB%d" % i, [256], BF16) for i in range(2)]
            rt = [A.alloc("rtB%d" % i, [128], F32) for i in range(4)]
            kfb = [A.alloc("kfb%d" % i, [2, 128], BF16) for i in range(2)]
            for n_, i in enumerate(tiles_all):
                bP = n_ % 2
                tok = slice(i * 128, (i + 1) * 128)
                for k in range(8):
                    P.mm(ps[:, bP, 0:384], hT[:, k, tok], WB[:, k, 0:384], k == 0, k == 7, [hT.k(i), WB.k()], [PSK(bP)])
                sg = stg[n_ % 2]
                if i >= 2:
                    xi = i - 2
                    v = ps[:, bP, 0:256].rearrange("p (g i j r) -> p g i j r", g=4, i=2, j=2)
                    a_, b_ = v[:, :, :, 0, :], v[:, :, :, 1, :]
                    sv = sg.ap.rearrange("p (g i j r) -> p g i j r", g=4, i=2, j=2)
                    cs = ropeB[:, xi, 0, :].rearrange("p (i r) -> p i r", i=2).unsqueeze(1).to_broadcast([128, 4, 2, 16])
                    sn = ropeB[:, xi, 1, :].rearrange("p (i r) -> p i r", i=2).unsqueeze(1).to_broadcast([128, 4, 2, 16])
                    r4 = [r_.ap.rearrange("p (g i r) -> p g i r", g=4, i=2) for r_ in rt]
                    RK = [PSK(bP), ropeB.k()]
                    P.tt("dve", r4[0], a_, cs, ALU.mult, RK, [rt[0].k()])
                    P.tt("dve", r4[1], b_, sn, ALU.mult, RK, [rt[1].k()])
                    P.tt("dve", r4[2], a_, sn, ALU.mult, RK, [rt[2].k()])
                    P.tt("dve", r4[3], b_, cs, ALU.mult, RK, [rt[3].k()])
                    P.tt("pool", sv[:, :, :, 0, :], r4[0], r4[1], ALU.subtract, [rt[0].k(), rt[1].k()], [sg.k()])
                    P.tt("pool", sv[:, :, :, 1, :], r4[2], r4[3], ALU.add, [rt[2].k(), rt[3].k()], [sg.k()])
                else:
                    P.cp("act", sg.ap, ps[:, bP, 0:256], [PSK(bP)], [sg.k()])
                if "vt" not in SKIP:
                    P.cp("dve", vt[:, i, :], ps[:, bP, 256:384], [PSK(bP)], [vt.k(i)])
                k3 = sg[:, 128:256].rearrange("p (h d) -> p h d", h=2)
                kd = kfb[n_ % 2]
                for d_ in (range(2) if "kd" not in SKIP else []):
                    P.tt("dve", kd[:, d_, :].rearrange("p (h d) -> p h d", h=2), k3,
                         kdec[:, d_, 2 * hp:2 * hp + 2].unsqueeze(2).to_broadcast([128, 2, 64]), ALU.mult,
                         [sg.k(), kdec.k()], [kd.k(d_)])
                bT = 2 + n_ % 2
                if "tr" not in SKIP:
                    for c in range(2):
                        P.tr(psb(bT)[:, c * 128:(c + 1) * 128], sg[:, c * 128:(c + 1) * 128], ident_b.ap,
                             [sg.k(), ident_b.k()], [PSK(bT)])
                    P.cp("act", qT[:, tok], psb(bT)[:, 0:128], [PSK(bT)], [qT.k(i)])
                    P.cp("act", kT[:, tok], psb(bT)[:, 128:256], [PSK(bT)], [kT.k(i)])
                if "U" in SKIP:
                    continue
                bU = 4 + n_ % 2
                P.mm(ps[:, bU, 0:128], kd[:, 0, :], vt[:, i, :], True, True, [kd.k(0), vt.k(i)], [PSK(bU)])
                P.mm(ps[:, bU, 128:256], kd[:, 1, :], vt[:, i, :], True, True, [kd.k(1), vt.k(i)], [PSK(bU)])
                uv = ps[:, bU, 0:256].rearrange("p (d n) -> p d n", d=2)
                for d_ in (range(2) if "Sc" not in SKIP else []):
                    nx = nxt_of[d_].get(i)
                    if nx is None:
                        continue
                    P.cp("dve", Sall[0:64, nx, d_, :], uv[0:64, d_, 0:64], [PSK(bU)], [Sall.k(nx, d_)])
                    P.cp("dve", Sall[64:128, nx, d_, :], uv[64:128, d_, 64:128], [PSK(bU)], [Sall.k(nx, d_)])
            if STOP[0] == "B1":
                A.release(mL)
                return
            P.op("dve", lambda e: e.memset(Sall[:, 0, 0, :], 0.0), [], [Sall.k(0, 0)])
            P.op("dve", lambda e: e.memset(Sall[:, 1, 1, :], 0.0), [], [Sall.k(1, 1)])
            for d_, order in ((0, fo), (1, bo)):
                for a_i in range(len(order) - 1):
                    cur, nxt = order[a_i], order[a_i + 1]
                    P.stt(Sall[:, nxt, d_, :], Sall[:, cur, d_, :], g128[:, hp, d_:d_ + 1], Sall[:, nxt, d_, :],
                          ALU.mult, ALU.add, [Sall.k(cur, d_), g128.k(), Sall.k(nxt, d_)], [Sall.k(nxt, d_)])
            P.cp("act", SallB.ap, Sall.ap, [Sall.k(i, d_) for i in range(NTILE) for d_ in range(2)], [SallB.k()])
            if STOP[0] == "B2":
                A.release(mL)
                return
            A.release(mB2)
            Pm = [A.alloc("Pm%d" % i, [256], BF16) for i in range(2)]
            qfb = [A.alloc("qfb%d" % i, [2, 128], BF16) for i in range(2)]
            sqB = A.alloc("sqB", [128], F32)
            ssB = A.alloc("ssB", [2], F32)
            yB = A.alloc("yB", [128], BF16)
            gsb = A.alloc("gsb", [128], F32)
            for n_, i in enumerate(out_tiles):
                tok = slice(i * 128, (i + 1) * 128)
                bA_ = n_ % 2
                bO = 2 + n_ % 2
                for h in range(2):
                    rs = slice(64 * h, 64 * h + 64)
                    P.mm(ps[:, h, 0:128], kT[rs, tok], qT[rs, tok], True, True,
                         [kT.k(i), qT.k(i)], [PSK(h)], tile_position=(64 * h, 0))
                pm = Pm[n_ % 2]
                qd = qfb[n_ % 2]
                for d_ in range(2):
                    P.tt("pool", qd[:, d_, :], qT[:, tok], TFB[:, hp, d_, :], ALU.mult, [qT.k(i), TFB.k()], [qd.k(d_)])
                for h in range(2):
                    P.tt("dve", pm[:, h * 128:(h + 1) * 128], ps[:, h, 0:128], DTt[:, 2 * hp + h, :], ALU.mult,
                         [PSK(h), DTt.k()], [pm.k()])
                for h in range(2):
                    rs = slice(64 * h, 64 * h + 64)
                    o = ps[:, bO, h * 64:(h + 1) * 64]
                    P.mm(o, pm[:, h * 128:(h + 1) * 128], vt[:, i, h * 64:(h + 1) * 64], True, False,
                         [pm.k(), vt.k(i)], [PSK(bO)])
                    P.mm(o, qd[rs, 0, :], SallB[rs, i, 0, :], False, False, [qd.k(0), SallB.k()], [PSK(bO)],
                         tile_position=(64 * h, 0))
                    P.mm(o, qd[rs, 1, :], SallB[rs, i, 1, :], False, True, [qd.k(1), SallB.k()], [PSK(bO)],
                         tile_position=(64 * h, 0))
                ov = ps[:, bO, 0:128]
                P.act(sqB.ap, ov, AF.Square, [PSK(bO)], [sqB.k()])
                P.op("dve", lambda e: e.tensor_reduce(out=ssB.ap, in_=sqB.ap.rearrange("p (h d) -> p h d", h=2), axis=AX.X, op=ALU.add),
                     [sqB.k()], [ssB.k()])
                P.act(ssB.ap, ssB.ap, AF.Sqrt, [ssB.k(), epsc.k()], [ssB.k()], scale=1.0 / 64, bias=epsc[:, 0:1])
                P.op("dve", lambda e: e.reciprocal(out=ssB.ap, in_=ssB.ap), [ssB.k()], [ssB.k()])
                s3 = sqB.ap.rearrange("p (h d) -> p h d", h=2)
                P.tt("dve", s3, ov.rearrange("p (h d) -> p h d", h=2), ssB.ap.unsqueeze(2).to_broadcast([128, 2, 64]),
                     ALU.mult, [PSK(bO), ssB.k()], [sqB.k()])
                bG = 4 + n_ % 2
                for k in range(8):
                    P.mm(ps[:, bG, 0:128], hT[:, k, tok], WB[:, k, 384:512], k == 0, k == 7, [hT.k(i), WB.k()], [PSK(bG)])
                P.act(gsb.ap, ps[:, bG, 0:128], AF.Silu, [PSK(bG)], [gsb.k()])
                P.tt("pool", yB.ap, sqB.ap, gsb.ap, ALU.mult, [sqB.k(), gsb.k()], [yB.k()])
                bT = 6 + n_ % 2
                P.tr(psb(bT)[:, 0:128], yB.ap, ident_b.ap, [yB.k(), ident_b.k()], [PSK(bT)])
                P.cp("act", ycs[1][:, hp, tok], psb(bT)[:, 0:128], [PSK(bT)], [ycs[1].k(hp, i)])
            A.release(mB)

        if STOP[0] == "B":
            A.release(mL)
            return
        tgs = ([(0, 256)] if need_ctx else []) + [(256 + 512 * q, 512) for q in range(4)]

        def tg_tiles(t0, n):
            return list(range(t0 // 128, (t0 + n) // 128))

        P.phase = "L%d_C" % l
        ycs[2] = A.alloc("ycC", [2, NT], BF16)
        mC = A.mark()
        PADC = 16
        LC = (TC + 2 * PADC) + (T + 2 * PADC)
        upad = A.alloc("upad", [2, LC], BF16)

        def cpos(t):
            return PADC + t if t < TC else (TC + 2 * PADC) + PADC + (t - TC)
        P.op("pool", lambda e: e.memset(upad.ap, 0.0), [], [upad.k(c, i) for c in range(2) for i in range(NTILE)])
        Dg = A.alloc("Dg", [2, 31, 128], BF16)
        for c in range(2):
            for k in range(31):
                P.ts("dve", Dg[:, c, k, :], ident_f.ap, cdw_w[:, c, k:k + 1], None, ALU.mult, None,
                     [ident_f.k()] + PFK, [Dg.k()])
        mC2 = A.mark()
        WC = A.alloc("WC", [8, 512], BF16)
        wcast(WC.ap, win_v[:, :, 768:1280], [WC.k()])
        sig = [A.alloc("sig%d" % i, [512], F32) for i in range(2)]
        for gi_, (t0, n) in enumerate(tgs):
            tl = tg_tiles(t0, n)
            for c in range(2):
                n_ = gi_ * 2 + c
                ba, bg = (0, 1) if n_ % 2 == 0 else (2, 3)
                for k in range(8):
                    P.mm(ps[:, ba, 0:n], WC[:, k, c * 128:(c + 1) * 128], hT[:, k, t0:t0 + n], k == 0, k == 7,
                         [WC.k()] + [hT.k(i) for i in tl], [PSK(ba)])
                for k in range(8):
                    P.mm(ps[:, bg, 0:n], WC[:, k, 256 + c * 128:256 + (c + 1) * 128], hT[:, k, t0:t0 + n], k == 0, k == 7,
                         [WC.k()] + [hT.k(i) for i in tl], [PSK(bg)])
                sgb = sig[n_ % 2]
                P.act(sgb[:, 0:n], ps[:, bg, 0:n], AF.Sigmoid, [PSK(bg)], [sgb.k()])
                p0 = cpos(t0)
                P.tt("dve", upad[:, c, p0:p0 + n], ps[:, ba, 0:n], sgb[:, 0:n], ALU.mult, [PSK(ba), sgb.k()],
                     [upad.k(c, i) for i in tl])
        A.release(mC2)
        y32 = A.alloc("y32", [2, 512], F32)
        ysq = A.alloc("ysq", [2, 512], F32)
        mean = A.alloc("mean", [512], F32)
        var = A.alloc("var", [512], F32)
        dtm = A.alloc("dtm", [512], F32)
        for gi_, (t0, n) in enumerate(tgs):
            tl = tg_tiles(t0, n)
            lo_t = max(tl[0] - 1, 0 if t0 < TC else 2)
            hi_t = min(tl[-1] + 1, 1 if t0 < TC else NTILE - 1)
            rtl = list(range(lo_t, hi_t + 1))
            p0 = cpos(t0)
            for c in range(2):
                bc_ = 4 + c
                for k in range(31):
                    P.mm(ps[:, bc_, 0:n], Dg[:, c, k, :], upad[:, c, p0 + k - 15:p0 + k - 15 + n], k == 0, k == 30,
                         [Dg.k()] + [upad.k(c, i) for i in rtl], [PSK(bc_)])
                P.act(y32[:, c, 0:n], ps[:, bc_, 0:n], AF.Identity, [PSK(bc_)] + PFK, [y32.k(c)], bias=cdw_b[:, c:c + 1])
                P.act(ysq[:, c, 0:n], ps[:, bc_, 0:n], AF.Square, [PSK(bc_)] + PFK, [ysq.k(c)], bias=cdw_b[:, c:c + 1])
            for c in range(2):
                P.mm(ps[:, 6, 0:n], ones_f.ap, y32[:, c, 0:n], c == 0, c == 1, [ones_f.k(), y32.k(c)], [PSK(6)])
            for c in range(2):
                P.mm(ps[:, 7, 0:n], ones_f.ap, ysq[:, c, 0:n], c == 0, c == 1, [ones_f.k(), ysq.k(c)], [PSK(7)])
            P.ts("dve", mean[:, 0:n], ps[:, 6, 0:n], 1.0 / 256, None, ALU.mult, None, [PSK(6)], [mean.k()])
            P.tt("dve", dtm[:, 0:n], mean[:, 0:n], mean[:, 0:n], ALU.mult, [mean.k()], [dtm.k()])
            P.stt(var[:, 0:n], ps[:, 7, 0:n], 1.0 / 256, dtm[:, 0:n], ALU.mult, ALU.subtract, [PSK(7), dtm.k()], [var.k()])
            P.act(var[:, 0:n], var[:, 0:n], AF.Ln, [var.k(), epsc.k()], [var.k()], bias=epsc[:, 0:1])
            P.act(var[:, 0:n], var[:, 0:n], AF.Exp, [var.k()], [var.k()], scale=-0.5)
            for c in range(2):
                P.tt("dve", dtm[:, 0:n], y32[:, c, 0:n], mean[:, 0:n], ALU.subtract, [y32.k(c), mean.k()], [dtm.k()])
                P.tt("dve", dtm[:, 0:n], dtm[:, 0:n], var[:, 0:n], ALU.mult, [dtm.k(), var.k()], [dtm.k()])
                P.act(ycs[2][:, c, t0:t0 + n], dtm[:, 0:n], AF.Silu, [dtm.k()] + PFK, [ycs[2].k(c, i) for i in tl],
                      scale=cln_g[:, c:c + 1], bias=cln_b[:, c:c + 1])
        A.release(mC)

        if STOP[0] == "C":
            A.release(mL)
            return
        P.phase = "L%d_D" % l
        ycs[3] = A.alloc("ycD", [2, NT], BF16)
        mD = A.mark()
        WD = A.alloc("WD", [8, 256], BF16)
        wcast(WD.ap, win_v[:, :, 1280:1536], [WD.k()])
        PWt = A.alloc("PWt", [2, 128], BF16)
        wcast(PWt.ap, poolw_d[l], [PWt.k()])
        PD = 16
        LD = (TC + 2 * PD) + (T + 2 * PD)
        seqs = ([(0, TC, 0)] if need_ctx else []) + [(TC, T, TC + 2 * PD)]
        pl = A.alloc("pl", [NT], BF16)
        for c in range(2):
            mDc = A.mark()
            ub = A.alloc("ub", [LD], F32)
            wb1 = A.alloc("wb1", [LD], F32)
            wb2 = A.alloc("wb2", [LD], F32)
            P.op("pool", lambda e, ub=ub: e.memset(ub.ap, 0.0), [], [ub.k()])
            for gi_, (t0, n) in enumerate(tgs):
                tl = tg_tiles(t0, n)
                bk = gi_ % 2
                for k in range(8):
                    P.mm(ps[:, bk, 0:n], WD[:, k, c * 128:(c + 1) * 128], hT[:, k, t0:t0 + n], k == 0, k == 7,
                         [WD.k()] + [hT.k(i) for i in tl], [PSK(bk)])
                base = PD + t0 if t0 < TC else (TC + 2 * PD) + PD + (t0 - TC)
                P.cp("act", ub[:, base:base + n], ps[:, bk, 0:n], [PSK(bk)], [ub.k()])
            for (tk0, n, pb) in seqs:
                z = pb + PD
                P.tt("dve", wb1[:, z - 8:z + n + 8], ub[:, z - 9:z + n + 7], ub[:, z - 8:z + n + 8], ALU.add, [ub.k()], [wb1.k()])
                P.tt("dve", wb2[:, z - 6:z + n + 6], wb1[:, z - 7:z + n + 5], wb1[:, z - 5:z + n + 7], ALU.add, [wb1.k()], [wb2.k()])
                finals = [wb1, wb2]
                wins = (2, 4)
                if c == 1:
                    P.tt("dve", wb1[:, z - 4:z + n + 4], wb2[:, z - 6:z + n + 2], wb2[:, z - 2:z + n + 6], ALU.add, [wb2.k()], [wb1.k()])
                    P.tt("dve", wb2[64:128, z:z + n], wb1[64:128, z - 4:z + n - 4], wb1[64:128, z + 4:z + n + 4], ALU.add,
                         [wb1.k()], [wb2.k()])
                    wins = (8, 16)
                for half in range(2):
                    sl = slice(64 * half, 64 * half + 64)
                    F_ = finals[half]
                    P.stt(pl[sl, tk0:tk0 + n], F_[sl, z:z + n], 1.0 / wins[half], ub[sl, z:z + n], ALU.mult, ALU.subtract,
                          [F_.k(), ub.k()], [pl.k(c)])
                    for e_, (a0, b0) in enumerate(((0, 8), (n - 8, n))):
                        P.tt("dve", wb1[sl, 0:8] if F_ is wb2 else wb2[sl, 0:8], F_[sl, z + a0:z + b0], pedge[sl, c, e_, :], ALU.mult,
                             [F_.k(), pedge.k()], [(wb1 if F_ is wb2 else wb2).k()])
                        P.tt("dve", pl[sl, tk0 + a0:tk0 + b0], wb1[sl, 0:8] if F_ is wb2 else wb2[sl, 0:8], ub[sl, z + a0:z + b0],
                             ALU.subtract, [(wb1 if F_ is wb2 else wb2).k(), ub.k()], [pl.k(c)])
            for gi_, (t0, n) in enumerate(tgs):
                tl = tg_tiles(t0, n)
                bk = 2 + gi_ % 2
                P.mm(ps[:, bk, 0:n], PWt[:, c, :], pl[:, t0:t0 + n], True, True, [PWt.k(), pl.k(c)], [PSK(bk)])
                P.act(ycs[3][:, c, t0:t0 + n], ps[:, bk, 0:n], AF.Copy, [PSK(bk)] + PFK, [ycs[3].k(c, i) for i in tl],
                      scale=pscale[:, c:c + 1])
            A.release(mDc)
        A.release(mD)

        if STOP[0] == "D":
            A.release(mL)
            return
        P.phase = "L%d_wout" % l
        def ycat_keys(i):
            return [ycs[m_].k(c_, i) for m_ in range(4) for c_ in range(2)]

        make_gbc(16)

        mO = A.mark()
        WO = A.alloc("WO", [8, D], BF16)
        wov = wout_d[l].rearrange("(k p) n -> p k n", p=128)
        for q in range(2):
            wcast(WO[:, 4 * q:4 * q + 4, :], wov[:, 4 * q:4 * q + 4, :], [WO.k()])
        etmp = [A.alloc("etmp%d" % i, [512], F32) for i in range(2)]

        def epilogue(bank, i, nh, gidx):
            et = etmp[nh]
            which = 1 if i < 2 else 0
            P.tt("dve", et.ap, ps[:, bank, :], gbc[:, which, nh * 512:(nh + 1) * 512], ALU.mult,
                 [PSK(bank), gbc.k(which)], [et.k()])
            P.tt("pool", res[:, i, nh * 512:(nh + 1) * 512], res[:, i, nh * 512:(nh + 1) * 512], et.ap, ALU.add,
                 [res.k(i), et.k()], [res.k(i)])

        xn2 = [A.alloc("xn2_%d" % i, [2, D], BF16) for i in range(2)]
        n2steps = norm_steps(A2, 24, out_tiles, xn2, 4)
        for n_, i in enumerate(out_tiles):
            tok = slice(i * 128, (i + 1) * 128)
            for nh in range(2):
                bank = (n_ % 2) * 2 + nh
                for k in range(8):
                    P.mm(ps[:, bank, :], ycs[k // 2][:, k % 2, tok], WO[:, k, nh * 512:(nh + 1) * 512], k == 0, k == 7,
                         ycat_keys(i) + [WO.k()], [PSK(bank)])
                epilogue(bank, i, nh, 0)
            if n_ % 2 == 1:
                P.phase = "L%d_norm2" % l
                n2steps[n_ // 2]()
                P.phase = "L%d_wout" % l
        A.release(mO)
        A.release(mL)

        if STOP[0] == "wout":
            return
        make_gbc(40)
        P.phase = "L%d_ffn" % l
        mF = A.mark()
        wupv = wup_d[l].rearrange("(k p) n -> p k n", p=128)
        wdnv = wdn_d[l].rearrange("(j p) n -> p j n", p=128)
        NG = max(FFN_GROUPS)
        Wa = [A.alloc("Wa%d" % i, [8, NG * 128], BF16) for i in range(2)]
        Wg = [A.alloc("Wg%d" % i, [8, NG * 128], BF16) for i in range(2)]
        Wd = [A.alloc("Wd%d" % i, [NG, D], BF16) for i in range(2)]
        actT = [A.alloc("actT%d" % i, [NG, NT], BF16) for i in range(2)]
        ua = [A.alloc("ua%d" % i, [512], F32) for i in range(2)]
        ug = [A.alloc("ug%d" % i, [512], F32) for i in range(2)]
        etmp = [A.alloc("etmpF%d" % i, [512], F32) for i in range(2)]
        fgs = ([(0, TC, 0, TC)] if need_ctx else [])
        xs = [0, 510, 1020, 1530, 2040, 2048]
        for q in range(5):
            fgs.append((TC + xs[q], TC + xs[q + 1], TC, NT))

        def load_up(gidx):
            ng = FFN_GROUPS[gidx]
            j0 = sum(FFN_GROUPS[:gidx])
            b = gidx % 2
            wcast(Wa[b][:, :, 0:ng * 128], wupv[:, :, j0 * 128:(j0 + ng) * 128], [Wa[b].k()])
            wcast(Wg[b][:, :, 0:ng * 128], wupv[:, :, DFF + j0 * 128:DFF + (j0 + ng) * 128], [Wg[b].k()])

        def load_dn(gidx):
            ng = FFN_GROUPS[gidx]
            j0 = sum(FFN_GROUPS[:gidx])
            b = gidx % 2
            wcast(Wd[b][:, 0:ng, :], wdnv[:, j0:j0 + ng, :], [Wd[b].k()])

        pcount = [0]

        def phaseA_steps(gidx):
            ng = FFN_GROUPS[gidx]
            j0 = sum(FFN_GROUPS[:gidx])
            b = gidx % 2
            aT_ = actT[b]
            steps = []
            for (s, e_, lo, hi) in fgs:
                def step(s=s, e_=e_, lo=lo, hi=hi):
                    cs_, ce_ = max(s - 1, lo), min(e_ + 1, hi)
                    ncol = ce_ - cs_
                    nv = e_ - s
                    off = s - cs_
                    tl = list(range(cs_ // 128, (ce_ - 1) // 128 + 1))
                    wtl = list(range(s // 128, (e_ - 1) // 128 + 1))
                    for jl in range(ng):
                        pp = pcount[0] % 2
                        pcount[0] += 1
                        ba, bg = (0, 1) if pp == 0 else (2, 3)
                        ja, jg = j0 + jl, 22 + j0 + jl
                        for (bank, W_) in ((ba, Wa[b]), (bg, Wg[b])):
                            for k in range(8):
                                P.mm(ps[:, bank, 0:ncol], W_[:, k, jl * 128:(jl + 1) * 128], hT[:, k, cs_:ce_], k == 0, k == 7,
                                     [W_.k()] + [hT.k(i) for i in tl], [PSK(bank)])
                        for (bank, u_, jj) in ((ba, ua[pp], ja), (bg, ug[pp], jg)):
                            P.act(u_[:, 0:nv], ps[:, bank, off:off + nv], AF.Identity, [PSK(bank)] + PFK, [u_.k()],
                                  scale=fdw_w[:, jj, 1:2], bias=fdw_b[:, jj:jj + 1])
                            tl0 = max(s, lo + 1)
                            P.stt(u_[:, tl0 - s:nv], ps[:, bank, tl0 - 1 - cs_:e_ - 1 - cs_], fdw_w[:, jj, 0:1], u_[:, tl0 - s:nv],
                                  ALU.mult, ALU.add, [PSK(bank), u_.k()] + PFK, [u_.k()])
                            tr1 = min(e_, hi - 1)
                            P.stt(u_[:, 0:tr1 - s], ps[:, bank, s + 1 - cs_:tr1 + 1 - cs_], fdw_w[:, jj, 2:3], u_[:, 0:tr1 - s],
                                  ALU.mult, ALU.add, [PSK(bank), u_.k()] + PFK, [u_.k()])
                        P.act(ug[pp][:, 0:nv], ug[pp][:, 0:nv], AF.Silu, [ug[pp].k()], [ug[pp].k()])
                        P.tt("pool", aT_[:, jl, s:e_], ua[pp][:, 0:nv], ug[pp][:, 0:nv], ALU.mult, [ua[pp].k(), ug[pp].k()],
                             [aT_.k(jl, i) for i in wtl])
                steps.append(step)
            return steps

        def phaseB_steps(gidx):
            ng = FFN_GROUPS[gidx]
            b = gidx % 2
            aT_ = actT[b]
            steps = []
            for n_, i in enumerate(out_tiles):
                def step(n_=n_, i=i):
                    tok = slice(i * 128, (i + 1) * 128)
                    for nh in range(2):
                        bank = 4 + (n_ % 2) * 2 + nh
                        for jl in range(ng):
                            P.mm(ps[:, bank, :], aT_[:, jl, tok], Wd[b][:, jl, nh * 512:(nh + 1) * 512], jl == 0, jl == ng - 1,
                                 [aT_.k(jl, i), Wd[b].k()], [PSK(bank)])
                        epilogue(bank, i, nh, 1)
                steps.append(step)
            return steps

        nG = len(FFN_GROUPS)
        load_up(0)
        load_dn(0)
        prevB = []
        for gidx in range(nG):
            if gidx + 1 < nG:
                load_up(gidx + 1)
            a_steps = phaseA_steps(gidx)
            nb, na = len(prevB), len(a_steps)
            bi_ = 0
            for ai, st in enumerate(a_steps):
                st()
                tgt = (ai + 1) * nb // na
                while bi_ < tgt:
                    prevB[bi_]()
                    bi_ += 1
            if gidx + 1 < nG:
                load_dn(gidx + 1)
            prevB = phaseB_steps(gidx)
        for st in prevB:
            st()
        A.release(mF)

    for l in range(n_layers):
        layer(l)

    m = A.mark()
    fg = A.alloc("fg", [D], F32)
    P.dma(fg.ap, fg_d, writes=[fg.k()])
    ob = [A.alloc("ob%d" % i, [D], F32) for i in range(2)]
    ov = out_d.rearrange("(i p) d -> p i d", p=128)
    for n_, i in enumerate(range(2, NTILE)):
        P.act(ob[n_ % 2].ap, res[:, i, :], AF.Square, [res.k(i)], [ob[n_ % 2].k(), ss.k(i)], accum_out=ss[:, i:i + 1])
        P.act(rstd[:, i:i + 1], ss[:, i:i + 1], AF.Sqrt, [ss.k(i), epsc.k()], [rstd.k(i)], scale=1.0 / D, bias=epsc[:, 0:1])
        P.op("dve", lambda e, i=i: e.reciprocal(out=rstd[:, i:i + 1], in_=rstd[:, i:i + 1]), [rstd.k(i)], [rstd.k(i)])
        o = ob[n_ % 2]
        P.stt(o.ap, res[:, i, :], rstd[:, i:i + 1], fg.ap, ALU.mult, ALU.mult, [res.k(i), rstd.k(i), fg.k()], [o.k()])
        P.dma(ov[:, i - 2, :], o.ap, reads=[o.k()])
    if debug is not None:
        debug_fn[0](P, locals(), dbg_d)
    A.release(m)
    P.emit()
    es.close()
    print("program: %d instrs, arena peak %d KB" % (P.n, A.peak // 1024), flush=True)
    return nc


debug_fn = [None]
STOP = [None]
SCOPES = [False]
import os as _os
SKIP = set((_os.environ.get('BSKIP') or '').split(','))


def _fm(v):
    v = np.asarray(v, np.float32)
    return np.ascontiguousarray(v.reshape(-1, 128).T)


def host_consts():
    c = {}
    c["ident"] = np.eye(128, dtype=np.float32)
    t = np.arange(T)
    rows = (t // 64).astype(np.float64)
    cols = (t % 64).astype(np.float64)
    for name, q in (("ropeA", 8), ("ropeB", 16)):
        inv = 10000.0 ** (-np.arange(q, dtype=np.float64) / q)
        ang = np.stack([rows[:, None] * inv, cols[:, None] * inv], axis=1)
        tab = np.stack([np.cos(ang), np.sin(ang)], axis=1)
        tab = tab.reshape(16, 128, 2, 2 * q).transpose(1, 0, 2, 3)
        c[name] = np.ascontiguousarray(tab).astype(np.float32)
    j = np.arange(128)[:, None].astype(np.float32)
    i = np.arange(128)[None, :].astype(np.float32)
    relm = i - j
    c["rel"] = np.ascontiguousarray(np.stack([relm, -relm, (relm >= 0).astype(np.float32),
                                               (relm <= 0).astype(np.float32)], axis=1)).astype(np.float32)
    ii = np.arange(128, dtype=np.float32)
    c["iot"] = np.ascontiguousarray(np.broadcast_to(np.stack([ii + 1, 128 - ii], 0)[None], (128, 2, 128))).astype(np.float32)
    p = np.arange(128, dtype=np.float32)
    c["pcol"] = np.ascontiguousarray(np.stack([127 - p, p], axis=1)).astype(np.float32)
    pe = np.zeros((128, 2, 2, 8), np.float32)
    wins = (2, 4, 8, 16)
    n = 1 << 20
    for g, w in enumerate(wins):
        cc, half = g // 2, g % 2
        for e in range(8):
            tt = e
            cnt = min(tt + w // 2, n) - max(tt - w // 2, 0)
            pe[64 * half:64 * half + 64, cc, 0, e] = 1.0 / cnt
            tt = n - 8 + e
            cnt = min(tt + w // 2, n) - max(tt - w // 2, 0)
            pe[64 * half:64 * half + 64, cc, 1, e] = 1.0 / cnt
    c["pooledge"] = pe
    return c


def host_prepare(inputs):
    f = lambda k: np.asarray(inputs[k], np.float32)
    shared = {}
    for k in ("w_mod", "w_in", "w_out"):
        shared[k] = np.ascontiguousarray(f(k))
    shared["w_up"] = np.ascontiguousarray(f("ffn_w_up"))
    shared["w_down"] = np.ascontiguousarray(f("ffn_w_down"))
    shared["bmodT"] = np.ascontiguousarray(np.stack([_fm(f("b_mod")[l]) for l in range(DEPTH)], axis=1))
    pfm = np.zeros((128, DEPTH, NPF), np.float32)
    pbc = np.zeros((128, DEPTH, NPB), np.float32)
    poolw = np.zeros((DEPTH, 128, 2, 128), np.float32)
    for l in range(DEPTH):
        pfm[:, l, 0:8] = _fm(f("norm1_g")[l])
        pfm[:, l, 8:16] = _fm(f("norm2_g")[l])
        cw = f("conv_dw_w")[l]
        pfm[:, l, 16:78] = cw.T.reshape(2, 128, 31).transpose(1, 0, 2).reshape(128, 62)
        pfm[:, l, 78:80] = _fm(f("conv_dw_b")[l])
        pfm[:, l, 80:82] = _fm(f("conv_ln_g")[l])
        pfm[:, l, 82:84] = _fm(f("conv_ln_b")[l])
        pfm[:, l, 84:86] = _fm(f("pool_scale")[l])
        fw_ = f("ffn_dw_w")[l]
        pfm[:, l, 86:218] = fw_.T.reshape(44, 128, 3).transpose(1, 0, 2).reshape(128, 132)
        pfm[:, l, 218:262] = _fm(f("ffn_dw_b")[l])
        pbc[:, l, 0:64] = f("diff_subln_g")[l][None, :]
        pbc[:, l, 64:192] = f("diff_lambda")[l].reshape(1, 128)
        pbc[:, l, 192:200] = f("ret_decay")[l].reshape(1, 8)
        pw = f("pool_w")[l]
        for g in range(4):
            cc, half = g // 2, g % 2
            poolw[l, 64 * half:64 * half + 64, cc, 64 * half:64 * half + 64] = pw[g]
    shared["pfm"] = pfm
    shared["pbc"] = pbc
    shared["poolw"] = poolw
    shared["final_g_bc"] = np.ascontiguousarray(np.broadcast_to(f("final_g")[None, :], (128, D))).astype(np.float32)
    shared.update(host_consts())
    x, c, ctx, c_ctx = f("x"), f("c"), f("ctx"), f("c_ctx")
    in_maps = []
    for b in range(x.shape[0]):
        m = dict(shared)
        m["x"] = np.ascontiguousarray(x[b])
        m["ctx"] = np.ascontiguousarray(ctx[b])
        cc = np.stack([c[b], c_ctx], axis=0)
        m["ccT"] = np.ascontiguousarray(cc.reshape(2, 8, 128).transpose(2, 1, 0))
        in_maps.append(m)
    return in_maps


_NC_CACHE = {}


def kernel(**inputs):
    in_maps = host_prepare(inputs)
    if "nc" not in _NC_CACHE:
        _NC_CACHE["nc"] = build_program(DEPTH)
    nc = _NC_CACHE["nc"]
    res = run_bass_kernel_spmd(nc, in_maps, core_ids=list(range(len(in_maps))))
    out = np.stack([np.asarray(r["out"], np.float32) for r in res.results], axis=0)
    return out
```
